# Optimizing a Trainium2 kernel written in Bass

```python
import math
import jax, jax.numpy as jnp
from jax import lax
import numpy as np

D_MODEL = 1024
BATCH = 8
SEQ = 4096
DEPTH = 4

HEAD_DIM = 64
A_Q_HEADS = 4
A_KV_HEADS = 2
A_RADIUS = 128
B_WIDTH = 256
SHORT_CONV = 3
C_HEADS = 4
C_Q_RANK = 256
C_KV_RANK = 128
C_NOPE_DIM = 64
C_ROPE_DIM = 32
C_V_DIM = 64
C_Q_BLOCK = 128
ROPE_THETA = 10000.0
D_HEADS = 4
D_PATTERNS = ((128, 1), (512, 4), (2048, 16))
REL_BUCKETS = 32
REL_MAX_DISTANCE = 1024
N_BIAS_HEADS = A_Q_HEADS + D_HEADS * len(D_PATTERNS)
MIX_WIDTH = A_Q_HEADS * HEAD_DIM + B_WIDTH + C_HEADS * C_V_DIM + D_HEADS * HEAD_DIM
IN_SPLITS = (A_Q_HEADS * HEAD_DIM, A_KV_HEADS * HEAD_DIM, A_KV_HEADS * HEAD_DIM,
             B_WIDTH, B_WIDTH, B_WIDTH,
             C_Q_RANK, C_KV_RANK, C_ROPE_DIM) + (D_HEADS * HEAD_DIM,) * (3 * len(D_PATTERNS))
IN_WIDTH = sum(IN_SPLITS)
D_FF = ((8 * D_MODEL // 3 + 127) // 128) * 128
FFN_CONV = 3
EPS = 1e-6
NEG = -1e30

kernel_name = 'hybrid_parallel_mixer_encoder'


def rms_norm(x, g):
    x32 = x.astype(jnp.float32)
    y = x32 * lax.rsqrt(jnp.mean(x32 * x32, axis=-1, keepdims=True) + EPS)
    return (y * g.astype(jnp.float32)).astype(x.dtype)


def dwconv3(x, w):
    xp = jnp.pad(x, ((0, 0), (1, 1), (0, 0)))
    return xp[:, :-2] * w[0] + xp[:, 1:-1] * w[1] + xp[:, 2:] * w[2]


def t5_bucket(rel):
    half = REL_BUCKETS // 2
    max_exact = half // 2
    n = jnp.abs(rel)
    n_f = jnp.maximum(n, 1).astype(jnp.float32)
    large = max_exact + (jnp.log(n_f / max_exact) / math.log(REL_MAX_DISTANCE / max_exact)
                         * (half - max_exact)).astype(jnp.int32)
    large = jnp.minimum(large, half - 1)
    return jnp.where(rel > 0, half, 0) + jnp.where(n < max_exact, n, large)


def band_bias(table, radius, stride):
    qi = jnp.arange(radius)[:, None]
    kj = jnp.arange(3 * radius)[None, :]
    rel = (kj - radius - qi) * stride
    return jnp.transpose(table[t5_bucket(rel)], (2, 0, 1))


def banded_attention(q, k, v, bias, sink, return_lse):
    n, L, H, dh = q.shape
    hk = k.shape[2]
    g = H // hk
    R = bias.shape[1]
    nb = -(-L // R)
    Lp = nb * R
    qb = jnp.pad(q, ((0, 0), (0, Lp - L), (0, 0), (0, 0))).reshape(n, nb, R, hk, g, dh)

    def windows(t):
        tp = jnp.pad(t, ((0, 0), (R, Lp - L + R), (0, 0), (0, 0))).reshape(n, nb + 2, R, hk, dh)
        return jnp.concatenate([tp[:, :-2], tp[:, 1:-1], tp[:, 2:]], axis=2)

    kw, vw = windows(k), windows(v)
    s = jnp.einsum('nbqhgd,nbjhd->nbhgqj', qb, kw, preferred_element_type=jnp.float32) * (dh ** -0.5)
    s = s + bias.astype(jnp.float32).reshape(hk, g, R, 3 * R)
    qpos = jnp.arange(Lp).reshape(nb, R, 1)
    kpos = qpos[:, :1] - R + jnp.arange(3 * R)
    valid = (jnp.abs(kpos - qpos) <= R) & (kpos >= 0) & (kpos < L)
    s = jnp.where(valid[None, :, None, None], s, NEG)
    m = jnp.max(s, axis=-1)
    if sink is not None:
        sk = sink.astype(jnp.float32).reshape(1, 1, hk, g, 1)
        m = jnp.maximum(m, sk)
    p = jnp.exp(s - m[..., None])
    den = jnp.sum(p, axis=-1)
    if sink is not None:
        den = den + jnp.exp(sk - m)
    o = jnp.einsum('nbhgqj,nbjhd->nbqhgd', p, vw.astype(jnp.float32)) / jnp.moveaxis(den, -1, 2)[..., None]
    o = o.reshape(n, Lp, H, dh)[:, :L].astype(q.dtype)
    if not return_lse:
        return o
    lse = jnp.moveaxis(m + jnp.log(den), -1, 2).reshape(n, Lp, H)[:, :L]
    return o, lse


def to_strided(t, d):
    b, s, h, dh = t.shape
    return jnp.swapaxes(t.reshape(b, s // d, d, h, dh), 1, 2).reshape(b * d, s // d, h, dh)


def from_strided(t, batch, d):
    L = t.shape[1]
    rest = t.shape[2:]
    return jnp.swapaxes(t.reshape((batch, d, L) + rest), 1, 2).reshape((batch, L * d) + rest)


def rope(t, cos, sin):
    t1, t2 = jnp.split(t, 2, axis=-1)
    cos = cos.astype(t.dtype)
    sin = sin.astype(t.dtype)
    return jnp.concatenate([t1 * cos - t2 * sin, t2 * cos + t1 * sin], axis=-1)


def mla_attention(q_nope, q_rope, k_nope, k_rope, v):
    b, s, h, _ = q_nope.shape
    nq = s // C_Q_BLOCK
    scale = (C_NOPE_DIM + C_ROPE_DIM) ** -0.5

    def block(args):
        qn, qr = args
        sc = (jnp.einsum('bqhd,bkhd->bhqk', qn, k_nope, preferred_element_type=jnp.float32)
              + jnp.einsum('bqhr,bkr->bhqk', qr, k_rope, preferred_element_type=jnp.float32)) * scale
        p = jax.nn.softmax(sc, axis=-1)
        return jnp.einsum('bhqk,bkhd->bqhd', p.astype(v.dtype), v)

    qn_b = jnp.swapaxes(q_nope.reshape(b, nq, C_Q_BLOCK, h, C_NOPE_DIM), 0, 1)
    qr_b = jnp.swapaxes(q_rope.reshape(b, nq, C_Q_BLOCK, h, C_ROPE_DIM), 0, 1)
    o = lax.map(block, (qn_b, qr_b))
    return jnp.swapaxes(o, 0, 1).reshape(b, s, h * C_V_DIM)


def setup_inputs(seed: int = 0) -> dict:
    key = jax.random.key(seed)
    ks = jax.random.split(key, 18)
    f32 = jnp.float32

    def nrm(k, shape, scale):
        return jax.random.normal(k, shape, f32) * scale

    L = DEPTH
    return {
        'x': nrm(ks[0], (BATCH, SEQ, D_MODEL), 1.0),
        'c': nrm(ks[1], (BATCH, D_MODEL), 1.0),
        'positions': jnp.tile(jnp.arange(SEQ, dtype=jnp.int32)[None, :], (BATCH, 1)),
        'rel_bias': nrm(ks[2], (REL_BUCKETS, N_BIAS_HEADS), 0.5),
        'w_mod': nrm(ks[3], (L, D_MODEL, 6 * D_MODEL), 0.5 * D_MODEL ** -0.5),
        'b_mod': nrm(ks[4], (L, 6 * D_MODEL), 0.01),
        'norm_g': 1.0 + nrm(ks[5], (L, 4, D_MODEL), 0.1),
        'w_in': nrm(ks[6], (L, D_MODEL, IN_WIDTH), D_MODEL ** -0.5),
        'a_sink': nrm(ks[7], (L, A_Q_HEADS), 0.5),
        'b_conv': nrm(ks[8], (L, SHORT_CONV, B_WIDTH), SHORT_CONV ** -0.5),
        'c_norm_q': 1.0 + nrm(ks[9], (L, C_Q_RANK), 0.1),
        'c_norm_kv': 1.0 + nrm(ks[10], (L, C_KV_RANK), 0.1),
        'c_w_uq': nrm(ks[11], (L, C_Q_RANK, C_HEADS * (C_NOPE_DIM + C_ROPE_DIM)), C_Q_RANK ** -0.5),
        'c_w_ukv': nrm(ks[12], (L, C_KV_RANK, C_HEADS * (C_NOPE_DIM + C_V_DIM)), C_KV_RANK ** -0.5),
        'w_out': nrm(ks[13], (L, MIX_WIDTH, D_MODEL), MIX_WIDTH ** -0.5),
        'w_up': nrm(ks[14], (L, D_MODEL, 2 * D_FF), D_MODEL ** -0.5),
        'ffn_conv': nrm(ks[15], (L, FFN_CONV, 2 * D_FF), FFN_CONV ** -0.5),
        'w_down': nrm(ks[16], (L, D_FF, D_MODEL), D_FF ** -0.5),
    }


def reference(x, c, positions, rel_bias, w_mod, b_mod, norm_g, w_in, a_sink, b_conv, c_norm_q, c_norm_kv,
              c_w_uq, c_w_ukv, w_out, w_up, ffn_conv, w_down):
    b, s, _ = x.shape
    bias_a = band_bias(rel_bias[:, :A_Q_HEADS], A_RADIUS, 1)
    bias_d = [band_bias(rel_bias[:, A_Q_HEADS + i * D_HEADS:A_Q_HEADS + (i + 1) * D_HEADS], (w // 2) // d, d)
              for i, (w, d) in enumerate(D_PATTERNS)]
    half = C_ROPE_DIM // 2
    inv_freq = ROPE_THETA ** (-jnp.arange(half, dtype=jnp.float32) / half)
    ang = positions.astype(jnp.float32)[..., None] * inv_freq
    cos, sin = jnp.cos(ang), jnp.sin(ang)
    split_idx = [int(i) for i in np.cumsum(IN_SPLITS)[:-1]]
    c_act = jax.nn.silu(c)

    for l in range(DEPTH):
        mod = (c_act @ w_mod[l] + b_mod[l])[:, None, :]
        sh1, sc1, g1, sh2, sc2, g2 = jnp.split(mod, 6, axis=-1)

        h = rms_norm(x, norm_g[l, 0]) * (1 + sc1) + sh1
        parts = jnp.split(h @ w_in[l], split_idx, axis=-1)
        aq, ak, av, bb, bc, bh, cq, ckv, ckr = parts[:9]
        dqkv = parts[9:]

        oa = banded_attention(aq.reshape(b, s, A_Q_HEADS, HEAD_DIM), ak.reshape(b, s, A_KV_HEADS, HEAD_DIM),
                              av.reshape(b, s, A_KV_HEADS, HEAD_DIM), bias_a, a_sink[l], False).reshape(b, s, -1)

        ob = bb * dwconv3(bc * bh, b_conv[l])

        q = (rms_norm(cq, c_norm_q[l]) @ c_w_uq[l]).reshape(b, s, C_HEADS, C_NOPE_DIM + C_ROPE_DIM)
        kv = (rms_norm(ckv, c_norm_kv[l]) @ c_w_ukv[l]).reshape(b, s, C_HEADS, C_NOPE_DIM + C_V_DIM)
        q_rope = rope(q[..., C_NOPE_DIM:], cos[:, :, None, :], sin[:, :, None, :])
        k_rope = rope(ckr, cos, sin)
        oc = mla_attention(q[..., :C_NOPE_DIM], q_rope, kv[..., :C_NOPE_DIM], k_rope, kv[..., C_NOPE_DIM:])

        outs, lses = [], []
        for i, (w, d) in enumerate(D_PATTERNS):
            qd, kd, vd = (to_strided(t.reshape(b, s, D_HEADS, HEAD_DIM), d) for t in dqkv[3 * i:3 * i + 3])
            o, lse = banded_attention(qd, kd, vd, bias_d[i], None, True)
            outs.append(from_strided(o, b, d))
            lses.append(from_strided(lse, b, d))
        wts = jax.nn.softmax(jnp.stack(lses), axis=0)
        od = jnp.sum(jnp.stack(outs).astype(jnp.float32) * wts[..., None], axis=0).astype(x.dtype).reshape(b, s, -1)

        y = jnp.concatenate([oa, ob, oc, od], axis=-1) @ w_out[l]
        x = x + g1 * rms_norm(y, norm_g[l, 1])

        h = rms_norm(x, norm_g[l, 2]) * (1 + sc2) + sh2
        u = dwconv3(h @ w_up[l], ffn_conv[l])
        ug, uv = jnp.split(u, 2, axis=-1)
        y = (jax.nn.gelu(ug, approximate=True) * uv) @ w_down[l]
        x = x + g2 * rms_norm(y, norm_g[l, 3])
    return x
```

```python
import math
from contextlib import ExitStack

import numpy as np
import concourse.bass as bass
import concourse.mybir as mybir
from concourse.bass_utils import run_bass_kernel_spmd

F32 = mybir.dt.float32
BF16 = mybir.dt.bfloat16
I32 = mybir.dt.int32
ALU = mybir.AluOpType
AF = mybir.ActivationFunctionType

S = 4096
DM = 1024
NL = 4
DFF = 2816
EPS = 1e-6
NEGM = -30000.0
PADK = 64
SAME_ENGINE_SYNC = False
ENGS = ("pe", "act", "dve", "pool", "sp")


class Op:
    __slots__ = ("eng", "fn", "reads", "writes", "deps", "idx", "chan", "is_dma", "barrier", "phase")

    def __init__(self, eng, fn, reads, writes, chan=None, barrier=False):
        self.eng = eng
        self.fn = fn
        self.reads = reads
        self.writes = writes
        self.chan = chan
        self.is_dma = chan is not None
        self.barrier = barrier
        self.phase = ""


class Prog:
    def __init__(self):
        self.ops = []

    phase = ""

    def add(self, eng, fn, r=(), w=(), chan=None):
        op = Op(eng, fn, tuple(r), tuple(w), chan)
        op.phase = self.phase
        self.ops.append(op)

    def pe(self, fn, r=(), w=()):
        self.add("pe", fn, r, w)

    def act(self, fn, r=(), w=()):
        self.add("act", fn, r, w)

    def dve(self, fn, r=(), w=()):
        self.add("dve", fn, r, w)

    def pool(self, fn, r=(), w=()):
        self.add("pool", fn, r, w)

    def dma(self, chan, fn, r=(), w=(), eng="sp"):
        self.add(eng, fn, r, w, chan=chan)

    def barrier(self):
        for e in ENGS:
            self.ops.append(Op(e, None, (), (), None, barrier=True))

    def emit(self, nc, final_chans=()):
        ops = self.ops

        def stream(op):
            return ("dma", op.chan) if op.is_dma else op.eng

        last_writer, readers, last_in_stream = {}, {}, {}
        for i, op in enumerate(ops):
            op.idx = i
            if op.barrier:
                op.deps = set(last_in_stream.values())
                continue
            deps = set()
            for r in op.reads:
                lw = last_writer.get(r)
                if lw is not None:
                    deps.add(lw)
            for w in op.writes:
                lw = last_writer.get(w)
                if lw is not None:
                    deps.add(lw)
                deps.update(readers.get(w, ()))
            deps.discard(i)
            op.deps = deps
            for r in op.reads:
                readers.setdefault(r, []).append(i)
            for w in op.writes:
                last_writer[w] = i
                readers[w] = []
            last_in_stream[stream(op)] = i
        spos, cnt = {}, {}
        for op in ops:
            if op.barrier:
                continue
            s = stream(op)
            cnt[s] = cnt.get(s, 0) + 1
            spos[op.idx] = cnt[s]
        waited = {e: {} for e in ENGS}
        need = {}
        signaled = set()
        for op in ops:
            if op.is_dma:
                signaled.add(op.idx)
        for op in ops:
            e = op.eng
            best = {}
            for d in op.deps:
                dop = ops[d]
                s = stream(dop)
                if (not dop.is_dma) and dop.eng == e and not op.is_dma:
                    if e == "pe" or not SAME_ENGINE_SYNC or op.barrier:
                        continue
                if s not in best or spos[d] > spos[best[s]]:
                    best[s] = d
            lst = []
            for s, d in best.items():
                if waited[e].get(s, 0) >= spos[d]:
                    continue
                waited[e][s] = spos[d]
                lst.append((s, d))
                signaled.add(d)
            need[op.idx] = lst
        sigcount, sigval = {}, {}
        for op in ops:
            if op.idx in signaled:
                s = stream(op)
                sigcount[s] = sigcount.get(s, 0) + (16 if op.is_dma else 1)
                sigval[op.idx] = sigcount[s]
        streams = sorted(set(stream(ops[i]) for i in signaled), key=str)
        with ExitStack() as st:
            sems = {}
            for s in streams:
                nm = "s_" + (s if isinstance(s, str) else "d_" + str(s[1]))
                sems[s] = st.enter_context(nc.semaphore(nm))
            block = st.enter_context(nc.Block())
            per_eng = {e: [op for op in ops if op.eng == e] for e in ENGS}

            def run(e, eng):
                for op in per_eng[e]:
                    for (s, d) in need[op.idx]:
                        eng.wait_ge(sems[s], sigval[d])
                    if op.fn is None:
                        continue
                    ins = op.fn(eng)
                    if op.idx in sigval:
                        ins.then_inc(sems[stream(op)], 16 if op.is_dma else 1)
                if e == "sp":
                    for ch in final_chans:
                        s = ("dma", ch)
                        if s in sems:
                            eng.wait_ge(sems[s], sigcount[s])

            block.tensor(lambda eng: run("pe", eng))
            block.scalar(lambda eng: run("act", eng))
            block.vector(lambda eng: run("dve", eng))
            block.gpsimd(lambda eng: run("pool", eng))
            block.sync(lambda eng: run("sp", eng))
        self.n_ops = len(ops)
        self.n_sems = len(streams)


def _t5_bucket_np(rel):
    half, max_exact = 16, 8
    n = np.abs(rel)
    n_f = np.maximum(n, 1).astype(np.float32)
    val = (np.log(n_f / np.float32(max_exact)) / np.float32(math.log(1024 / max_exact))
           * np.float32(half - max_exact)).astype(np.float32)
    large = max_exact + val.astype(np.int32)
    large = np.minimum(large, half - 1)
    return np.where(rel > 0, half, 0) + np.where(n < max_exact, n, large)


A_OFF = dict(aq=0, ak=256, av=384, bb=512, bc=768, bh=1024, cq=1280, ckv=1536, ckr=1664, d0=1696)
D_PAT = ((128, 1), (512, 4), (2048, 16))
UA = 0
UD = 768
UCL = 768 + 2304
UB = UCL + 448
WUC = UB + 768


def _wu_cols():
    cols = []
    for h in range(4):
        cols += list(range(A_OFF["aq"] + h * 64, A_OFF["aq"] + h * 64 + 64))
        cols += list(range(A_OFF["av"] + (h // 2) * 64, A_OFF["av"] + (h // 2) * 64 + 64))
        cols += list(range(A_OFF["ak"] + (h // 2) * 64, A_OFF["ak"] + (h // 2) * 64 + 64))
    for h in range(4):
        for g in range(3):
            base = A_OFF["d0"] + g * 768
            for part in (0, 2, 1):
                cols += list(range(base + part * 256 + h * 64, base + part * 256 + h * 64 + 64))
    cols += list(range(A_OFF["cq"], A_OFF["cq"] + 256))
    cols += list(range(A_OFF["ckv"], A_OFF["ckv"] + 128))
    cols += list(range(A_OFF["ckr"], A_OFF["ckr"] + 32))
    cols += list(range(A_OFF["ckr"] + 16, A_OFF["ckr"] + 32)) + list(range(A_OFF["ckr"], A_OFF["ckr"] + 16))
    for ch in range(2):
        for nm in ("bb", "bc", "bh"):
            cols += list(range(A_OFF[nm] + ch * 128, A_OFF[nm] + ch * 128 + 128))
    assert len(cols) == WUC
    return np.array(cols, dtype=np.int64)


def _bias_tables(rel_bias):
    rb = np.asarray(rel_bias, dtype=np.float32)
    p = np.arange(128)[:, None]
    q = np.arange(128)[None, :]
    bA = np.empty((4, 128, 3, 128), np.float32)
    for h in range(4):
        for i in range(3):
            rel = 128 * i + p - 128 - q
            v = rb[_t5_bucket_np(rel), h]
            bA[h, :, i, :] = np.where(np.abs(rel) <= 128, v, np.float32(NEGM))
    bD = np.empty((12, 128, 6, 128), np.float32)
    for h in range(4):
        for g, (w, d) in enumerate(D_PAT):
            col = 4 + g * 4 + h
            rel1 = p - 64 - q
            rel2 = 64 + p - q
            t1 = np.where(np.abs(rel1) <= 64, rb[_t5_bucket_np(rel1 * d), col], np.float32(NEGM)).astype(np.float32)
            t2 = np.where(np.abs(rel2) <= 64, rb[_t5_bucket_np(rel2 * d), col], np.float32(NEGM)).astype(np.float32)
            t1e = np.where(p < 64, np.float32(NEGM), t1).astype(np.float32)
            t2e = np.where(p >= 64, np.float32(NEGM), t2).astype(np.float32)
            u = h * 3 + g
            for i, t in enumerate((t1, t2, t1e, t2, t1, t2e)):
                bD[u, :, i, :] = t
    return bA, bD


def build(nlayers=NL, debug=False, phases=None):
    PH = phases if phases is not None else {"P1", "LAT", "MLA", "A", "D", "B", "P3", "P4"}
    nc = bass.Bass("TRN2", target_bir_lowering=False)
    P = Prog()

    def din(name, shape, dt=F32):
        return nc.dram_tensor(name, list(shape), dt, kind="ExternalInput").ap()

    x_in = din("x", [S, DM])
    cT_in = din("cT", [128, 8])
    pos_in = din("pos", [1, S], I32)
    cst_in = din("cst", [128, 4])
    idn_in = din("idn", [128, 128])
    w_mod = din("w_mod", [NL, DM, 6 * DM])
    b_mod = din("b_mod", [NL, 1, 6 * DM])
    norm_g = din("norm_g", [NL, 1, 4 * DM])
    wu_in = din("wu", [NL, DM, WUC])
    sink_in = din("sink", [NL, 128, 4])
    bconv_in = din("bconv", [NL, 128, 2, 3])
    gq_in = din("gq", [NL, 128, 2])
    gkv_in = din("gkv", [NL, 128, 1])
    wq_in = din("wq", [NL, 256, 4 * 192])
    wkv_in = din("wkv", [NL, 128, 512])
    w_out = din("w_out", [NL, DM, DM])
    w_up = din("w_up", [NL, DM, 2 * DFF])
    fconv_in = din("fconv", [NL, 128, 44, 3])
    w_down = din("w_down", [NL, DFF, DM])
    biasA_in = din("biasA", [4, 128, 3, 128])
    biasD_in = din("biasD", [12, 128, 6, 128])
    out = nc.dram_tensor("out", [S, DM], F32, kind="ExternalOutput").ap()
    xmid = nc.dram_tensor("xmid", [S, DM], F32, kind="Internal").ap()
    mixd = nc.dram_tensor("mixd", [DM, S], BF16, kind="Internal").ap()
    dbg = None
    if debug:
        dbg = nc.dram_tensor("dbg", [DM, S], BF16, kind="ExternalOutput").ap()
    xname = {id(x_in): "xin", id(out): "out", id(xmid): "xmid"}

    cur = [(nc.sbuf_base + 63) // 64 * 64]
    top = nc.sbuf_top

    def alloc(name, shape, dt, at=None):
        per = int(np.prod(shape[1:])) * (2 if dt == BF16 else 4)
        per = (per + 63) // 64 * 64
        if at is None:
            off = cur[0]
            cur[0] += per
            assert cur[0] <= top, (name, cur[0], top)
        else:
            off = at
        return nc.alloc_sbuf_tensor_at(name, list(shape), dt, offset=off), off, per

    def T(name, shape, dt):
        return alloc(name, shape, dt)[0]

    identf = T("identf", [128, 128], F32)
    identb = T("identb", [128, 128], BF16)
    onesb = T("onesb", [128, 128], BF16)
    ones1 = T("ones1", [1, 128], F32)
    cst = T("cst", [128, 4], F32)
    epsb = T("epsb", [128, 1], F32)
    cact = T("cact", [128, 16], F32)
    s1T = T("s1T", [128, 8], F32)
    sh1T = T("sh1T", [128, 8], F32)
    s2T = T("s2T", [128, 8], F32)
    sh2T = T("sh2T", [128, 8], F32)
    G1B = T("G1B", [128, DM], F32)
    G2B = T("G2B", [128, DM], F32)
    esink = T("esink", [128, 4], F32)
    bconv = T("bconv", [128, 2, 3], F32)
    gq = T("gq", [128, 2], F32)
    gkv = T("gkv", [128, 1], F32)
    fconv = T("fconv", [128, 44, 3], F32)
    wkv = T("wkv", [128, 512], BF16)
    stat = T("stat", [128, 16], F32)
    xt = [T("xt%d" % i, [128, DM], F32) for i in range(2)]
    xn = [T("xn%d" % i, [128, DM], BF16) for i in range(2)]
    junk = T("junk", [128, DM], BF16)
    etmp = T("etmp", [128, DM], F32)
    wu = [T("wu%d" % i, [128, 8, 384], BF16) for i in range(2)]

    hT, offH, szH = alloc("hT", [128, 8, S], BF16)
    GT = alloc("GT", [128, 22, 1036], BF16, at=offH)[0]
    szGT = (22 * 1036 * 2 + 63) // 64 * 64
    HT2C = 1154
    hT2 = alloc("hT2", [128, 8, HT2C], BF16, at=offH + szGT)[0]
    assert szGT + 8 * HT2C * 2 <= szH
    wm = [alloc("wm%d" % i, [128, 8, 512], F32, at=offH + i * 16384)[0] for i in range(2)]
    wo = alloc("wo", [128, 8, DM], BF16, at=offH)[0]
    mixl = [alloc("mixl%d" % i, [128, 8, 512], BF16, at=offH + 16384 + i * 8192)[0] for i in range(2)]

    offU = cur[0]
    c = offU
    modB, _, sz = alloc("modB", [128, 6 * DM], F32, at=c); c += sz
    cactB, _, sz = alloc("cactB", [128, 8, 128], F32, at=c); c += sz
    rowv, _, sz = alloc("rowv", [1, 10 * DM], F32, at=c); c += sz
    tmpA, _, sz = alloc("tmpA", [128, DM], F32, at=c); c += sz
    endU_setup = c
    c = offU
    KCOLS = S + 2 * PADK
    QT, _, sz = alloc("QT", [128, S], BF16, at=c); c += sz
    KT, _, sz = alloc("KT", [128, KCOLS], BF16, at=c); c += sz
    VV, _, sz = alloc("VV", [128, 33, 128], BF16, at=c); c += sz
    OT, _, sz = alloc("OT", [128, S], BF16, at=c); c += sz
    acc, _, szacc = alloc("acc", [128, S], F32, at=c)
    cqnT, _, sz1 = alloc("cqnT", [128, 2, S], BF16, at=c)
    ckvnT, _, sz2 = alloc("ckvnT", [128, S], BF16, at=c + sz1)
    c += max(szacc, sz1 + sz2)
    biasT = []
    for i in range(2):
        t, _, sz = alloc("biasT%d" % i, [128, 6, 128], F32, at=c); c += sz
        biasT.append(t)
    sbt = []
    biasb = []
    for i in range(2):
        t, off_, sz = alloc("sbt%d" % i, [128, 512], F32, at=c); c += sz
        sbt.append(t)
        biasb.append(alloc("biasb%d" % i, [128, 6, 128], BF16, at=off_)[0])
    PT = []
    for i in range(3):
        t, _, sz = alloc("PT%d" % i, [128, 512], BF16, at=c); c += sz
        PT.append(t)
    dens, _, sz = alloc("dens", [128, 512], F32, at=c); c += sz
    rt1, _, sz = alloc("rt1", [128, 512], F32, at=c); c += sz
    rt2, _, sz = alloc("rt2", [128, 512], F32, at=c); c += sz
    sqb = []
    for i in range(3):
        t, _, sz = alloc("sqb%d" % i, [128, 512], BF16, at=c); c += sz
        sqb.append(t)
    wqh, _, sz = alloc("wqh", [128, 2, 192], BF16, at=c); c += sz
    TC, _, sz = alloc("TC", [128, S], BF16, at=c); c += sz
    TS, _, sz = alloc("TS", [128, S], BF16, at=c); c += sz
    endU_mix = c
    c = offU
    wdn, _, sz = alloc("wdn", [128, 22, DM], BF16, at=c); c += sz
    wup = []
    for i in range(2):
        t, _, sz = alloc("wup%d" % i, [128, 8, 1024], BF16, at=c); c += sz
        wup.append(t)
    tg, tv, tgg = [], [], []
    for i in range(2):
        t, _, sz = alloc("tg%d" % i, [128, 512], F32, at=c); c += sz
        tg.append(t)
        t, _, sz = alloc("tv%d" % i, [128, 512], F32, at=c); c += sz
        tv.append(t)
        t, _, sz = alloc("tgg%d" % i, [128, 512], BF16, at=c); c += sz
        tgg.append(t)
    fst = []
    for i in range(2):
        t, _, sz = alloc("fst%d" % i, [128, 2048], F32, at=c); c += sz
        fst.append(t)
    endU_ffn = c
    endU = max(endU_setup, endU_mix, endU_ffn)
    assert endU <= top, (endU, top, endU_setup - offU, endU_mix - offU, endU_ffn - offU)

    pb = [nc.alloc_psum_tensor("pb%d" % i, [128, 512], F32) for i in range(7)]
    pbT = nc.alloc_psum_tensor("pbT", [128, 8, 128], BF16)

    cnt = {"x": 0, "wu": 0, "uid": 0}

    def xtags(dr, t0, m):
        return [("X", xname[id(dr)], b) for b in range(t0 // 128, (t0 + m - 1) // 128 + 1)]

    P.dma("c_idn", lambda e: e.dma_start(out=identf[:], in_=idn_in), w=["identf"])
    P.dma("c_cst", lambda e: e.dma_start(out=cst[:], in_=cst_in), w=["cst"])
    P.dve(lambda e: e.tensor_copy(out=identb[:], in_=identf[:]), r=["identf"], w=["identb"])
    P.pool(lambda e: e.memset(onesb[:], 1.0), w=["onesb"])
    P.pool(lambda e: e.memset(ones1[:], 1.0), w=["ones1"])
    P.pool(lambda e: e.memset(epsb[:], EPS), w=["epsb"])
    P.dma("c_c", lambda e: e.dma_start(out=cact[:, 0:8], in_=cT_in), w=["cact"])
    P.act(lambda e: e.activation(out=cact[:, 8:16], in_=cact[:, 0:8], func=AF.Silu), r=["cact"], w=["cact"])
    P.barrier()

    def build_rope():
        posi = xt[1]
        posf = etmp
        TWO_PI = 2.0 * math.pi
        for piece in range(4 if "NOROPE" not in PH else 0):
            P.dma("c_pos", lambda e, piece=piece: e.dma_start(out=posi[0:1, :].bitcast(I32), in_=pos_in[:, piece * DM:(piece + 1) * DM]),
                  r=[("xt", 1)], w=[("xt", 1)])
            P.dve(lambda e: e.tensor_copy(out=posf[0:1, :], in_=posi[0:1, :].bitcast(I32)), r=[("xt", 1)], w=[("etmp", 0), ("etmp", 1)])
            for half in range(2):
                col0 = piece * DM + half * 512
                ang = pb[half][:]
                P.pe(lambda e, half=half, ang=ang: e.matmul(ang, lhsT=ones1[0:1, :], rhs=posf[0:1, half * 512:(half + 1) * 512],
                                                            start=True, stop=True), r=[("etmp", 0), ("etmp", 1), "ones1"], w=[("pb", half)])
                for which in (0, 1):
                    a = sbt[which]
                    ki = rt1 if which == 0 else rt2
                    P.dve(lambda e, a=a, ang=ang, which=which: e.tensor_scalar(
                        out=a[:], in0=ang, scalar1=cst[:, 0:1], scalar2=(0.5 * math.pi if which else 0.0),
                        op0=ALU.mult, op1=ALU.add), r=[("pb", half), "cst"], w=[("rr", which)])
                    P.dve(lambda e, a=a, ki=ki: e.tensor_scalar(out=ki[:].bitcast(I32), in0=a[:], scalar1=1.0 / TWO_PI,
                                                                scalar2=None, op0=ALU.mult), r=[("rr", which)], w=[("rk", which)])
                    P.dve(lambda e, ki=ki: e.tensor_copy(out=ki[:], in_=ki[:].bitcast(I32)), r=[("rk", which)], w=[("rk", which)])
                    P.dve(lambda e, a=a, ki=ki: e.scalar_tensor_tensor(out=a[:], in0=ki[:], scalar=-TWO_PI, in1=a[:],
                                                                       op0=ALU.mult, op1=ALU.add),
                          r=[("rk", which), ("rr", which)], w=[("rr", which)])
                    P.dve(lambda e, a=a: e.tensor_scalar(out=a[:], in0=a[:], scalar1=-3.14159, scalar2=3.14159,
                                                         op0=ALU.max, op1=ALU.min), r=[("rr", which)], w=[("rr", which)])
                    P.act(lambda e, a=a: e.activation(out=a[:], in_=a[:], func=AF.Sin), r=[("rr", which)], w=[("rr", which)])
                    if which == 0:
                        P.dve(lambda e, a=a, col0=col0: e.tensor_scalar(out=TS[:, col0:col0 + 512], in0=a[:], scalar1=cst[:, 1:2],
                                                                        scalar2=None, op0=ALU.mult), r=[("rr", 0), "cst"], w=["TS"])
                    else:
                        P.dve(lambda e, a=a, col0=col0: e.tensor_copy(out=TC[:, col0:col0 + 512], in_=a[:]), r=[("rr", 1)], w=["TC"])


    def load_x_block(src, t0, m):
        i = cnt["x"] % 2
        cnt["x"] += 1
        P.dma("ldx%d" % i, lambda e: e.dma_start(out=xt[i][0:m, :], in_=src[t0:t0 + m, :]), r=xtags(src, t0, m), w=[("xt", i)])
        return i

    def rstd_inplace(ap, m, n_inv, rtag):
        P.act(lambda e: e.activation(out=ap, in_=ap, func=AF.Sqrt, scale=n_inv, bias=epsb[0:m, 0:1]), r=[rtag, "epsb"], w=[rtag])
        P.dve(lambda e: e.reciprocal(out=ap, in_=ap), r=[rtag], w=[rtag])

    def norm_stage1(src, t0, m):
        i = load_x_block(src, t0, m)
        sc_ = i
        stg_ = "statn%d" % sc_
        P.pool(lambda e: e.memset(stat[:, sc_:sc_ + 1], 0.0), w=[stg_])
        P.act(lambda e: e.activation(out=junk[0:m, :], in_=xt[i][0:m, :], func=AF.Square, accum_out=stat[0:m, sc_:sc_ + 1]),
              r=[("xt", i), stg_], w=[stg_, "junk"])
        rstd_inplace(stat[0:m, sc_:sc_ + 1], m, 1.0 / DM, stg_)
        P.dve(lambda e: e.tensor_scalar(out=xn[i][0:m, :], in0=xt[i][0:m, :], scalar1=stat[0:m, sc_:sc_ + 1], scalar2=None,
                                        op0=ALU.mult), r=[("xt", i), stg_], w=[("xn", i)])
        return i

    def norm_stage2(i, m, sT, shT, dest, dcol, tagfn):
        pT = pbT[:]
        for k in range(8):
            P.pe(lambda e, k=k: e.transpose(out=pT[:, k, 0:m], in_=xn[i][0:m, k * 128:(k + 1) * 128], identity=identb[0:m, 0:m]),
                 r=[("xn", i), "identb"], w=[("pb", 7)])
        for k in range(8):
            P.act(lambda e, k=k: e.activation(out=dest[:, k, dcol:dcol + m], in_=pT[:, k, 0:m], func=AF.Identity,
                                              scale=sT[:, k:k + 1], bias=shT[:, k:k + 1]),
                  r=[("pb", 7), "modT"], w=[tagfn(0)])

    def norm_blocks(src, blocks, sT, shT, dest, dcol_fn, tag_fn):
        pend = None
        for j, bblk in enumerate(blocks):
            i = norm_stage1(src, bblk * 128, 128)
            if pend is not None:
                norm_stage2(*pend)
            pend = (i, 128, sT, shT, dest, dcol_fn(j, bblk), (lambda e_, j=j, bblk=bblk: tag_fn(j, bblk, e_)))
        if pend is not None:
            norm_stage2(*pend)

    def epilogue(psy0, psy1, ptag, src, dst, t0, m, GB):
        i = load_x_block(src, t0, m)
        so = 2 + 2 * (cnt["x"] % 2)
        st2 = "stat%d" % so
        P.act(lambda e: e.memzero(stat[:, so:so + 2]), w=[st2])
        P.act(lambda e: e.activation(out=junk[0:m, 0:512], in_=psy0[0:m, :], func=AF.Square, accum_out=stat[0:m, so:so + 1]),
              r=[ptag[0], st2], w=[st2, "junk"])
        P.act(lambda e: e.activation(out=junk[0:m, 512:1024], in_=psy1[0:m, :], func=AF.Square, accum_out=stat[0:m, so + 1:so + 2]),
              r=[ptag[1], st2], w=[st2, "junk"])
        P.dve(lambda e: e.tensor_tensor(out=stat[0:m, so:so + 1], in0=stat[0:m, so:so + 1], in1=stat[0:m, so + 1:so + 2], op=ALU.add),
              r=[st2], w=[st2])
        rstd_inplace(stat[0:m, so:so + 1], m, 1.0 / DM, st2)
        for hf, psy in enumerate((psy0, psy1)):
            sl = slice(hf * 512, (hf + 1) * 512)
            P.dve(lambda e, psy=psy, sl=sl: e.scalar_tensor_tensor(out=etmp[0:m, sl], in0=psy[0:m, :], scalar=stat[0:m, so:so + 1],
                                                                    in1=GB[0:m, sl], op0=ALU.mult, op1=ALU.mult),
                  r=[ptag[hf], st2, "GB"], w=[("etmp", hf)])
            P.pool(lambda e, sl=sl: e.tensor_tensor(out=xt[i][0:m, sl], in0=xt[i][0:m, sl], in1=etmp[0:m, sl], op=ALU.add),
                   r=[("etmp", hf), ("xt", i)], w=[("xt", i)])
        P.dma("stx%d" % i, lambda e: e.dma_start(out=dst[t0:t0 + m, :], in_=xt[i][0:m, :]), r=[("xt", i)], w=xtags(dst, t0, m), eng="pool")

    def stage_cast(chan_base, src_ap, nelem_shape, dst_ap, dtags):
        i = cnt["x"] % 2
        cnt["x"] += 1
        a_, b_ = nelem_shape
        sv = xt[i][:, 0:a_ * b_].rearrange("p (a b) -> p a b", a=a_) if a_ > 1 else xt[i][:, 0:b_]
        P.dma("ldx%d" % i, lambda e: e.dma_start(out=sv, in_=src_ap), w=[("xt", i)])
        P.pool(lambda e: e.tensor_copy(out=dst_ap, in_=sv), r=[("xt", i)], w=dtags)

    def load_wu(l, c0, ncols):
        wi = cnt["wu"] % 2
        cnt["wu"] += 1
        kstep = {192: 4, 384: 2, 64: 8}[ncols]
        for kk in range(0, 8, kstep):
            stage_cast("wu", wu_in[l][kk * 128:(kk + kstep) * 128, c0:c0 + ncols].rearrange("(k p) n -> p k n", p=128),
                       (kstep, ncols), wu[wi][:, kk:kk + kstep, 0:ncols], [("wu", wi)])
        return wi

    pj = [0]

    def proj_fm(wi, c0, M, tile_j, evac):
        b = pj[0] % 2
        pj[0] += 1
        ps = pb[b][:]
        for k in range(8):
            P.pe(lambda e, k=k: e.matmul(ps[0:M, :], lhsT=wu[wi][:, k, c0:c0 + M], rhs=hT[:, k, tile_j * 512:(tile_j + 1) * 512],
                                         start=(k == 0), stop=(k == 7)), r=[("wu", wi)], w=[("pb", b)])
        evac(ps, ("pb", b))

    def perm_dst(rows_ap, col_off, d, j):
        if d == 1:
            return rows_ap[:, col_off + j * 512: col_off + (j + 1) * 512]
        n = 512 // d
        return rows_ap[:, col_off:col_off + S].rearrange("p (r i) -> p r i", r=d)[:, :, j * n:(j + 1) * n]

    def perm_src(ps_rows, d):
        if d == 1:
            return ps_rows
        return ps_rows.rearrange("p (i r) -> p r i", r=d)

    def v_segments(m, d, koff):
        Ld = S // d
        q0 = koff + 128 * m
        segs = []
        for half in range(2):
            qs = q0 + 64 * half
            if qs < 0 or qs >= S:
                continue
            r, i = divmod(qs, Ld)
            tok = i * d + r
            if segs and segs[-1][1] == 64 and segs[-1][0] == 0 and segs[-1][2] + 64 * d == tok and (qs % Ld) != 0:
                segs[-1] = (0, 128, segs[-1][2])
            else:
                segs.append((64 * half, 64, tok))
        return segs

    vslot = [0]

    def proj_v(wi, c0, d, koff, nblk):
        segs = []
        for m in range(nblk):
            for (row0, nr, tok) in v_segments(m, d, koff):
                segs.append((m, row0, nr, tok))
        for g0_ in range(0, len(segs), 8):
            grp = segs[g0_:g0_ + 8]
            gi = vslot[0]
            vslot[0] += 1
            bank = (2, 4)[gi % 2]
            ps = pb[bank][:]
            for si, (m, row0, nr, tok) in enumerate(grp):
                cs = si * 64
                tsl = slice(tok, tok + (nr - 1) * d + 1, d)
                for k in range(8):
                    P.pe(lambda e, k=k, cs=cs, nr=nr, tsl=tsl, ps=ps: e.matmul(
                        ps[0:nr, cs:cs + 64], lhsT=hT[:, k, tsl], rhs=wu[wi][:, k, c0:c0 + 64], start=(k == 0), stop=(k == 7)),
                        r=[("wu", wi)], w=[("pb", bank)])
            for si, (m, row0, nr, tok) in enumerate(grp):
                cs = si * 64
                if gi % 2 == 0:
                    P.act(lambda e, cs=cs, nr=nr, row0=row0, m=m, ps=ps: e.copy(out=VV[row0:row0 + nr, m, 0:64], in_=ps[0:nr, cs:cs + 64]),
                          r=[("pb", bank), "VVi"], w=[("VV", m, row0)] + ([("VV", m, 64)] if nr == 128 else []))
                else:
                    P.dve(lambda e, cs=cs, nr=nr, row0=row0, m=m, ps=ps: e.tensor_copy(out=VV[row0:row0 + nr, m, 0:64], in_=ps[0:nr, cs:cs + 64]),
                          r=[("pb", bank), "VVi"], w=[("VV", m, row0)] + ([("VV", m, 64)] if nr == 128 else []))

    def vtags(m):
        return [("VV", m, 0), ("VV", m, 64)]

    def finalize_block(ops, otag, ncols, dst_ap, sink_ap):
        if sink_ap is not None:
            P.dve(lambda e: e.tensor_scalar(out=dens[0:64, 0:ncols], in0=ops[64:128, 0:ncols], scalar1=sink_ap, scalar2=None,
                                            op0=ALU.add), r=[otag, "esink"], w=["dens"])
        else:
            P.dve(lambda e: e.tensor_copy(out=dens[0:64, 0:ncols], in_=ops[64:128, 0:ncols]), r=[otag], w=["dens"])
        P.dve(lambda e: e.reciprocal(out=dens[0:64, 0:ncols], in_=dens[0:64, 0:ncols]), r=["dens"], w=["dens"])
        P.dve(lambda e: e.tensor_tensor(out=dst_ap, in0=ops[0:64, 0:ncols], in1=dens[0:64, 0:ncols], op=ALU.mult),
              r=[otag, "dens"], w=["OT"])

    def store_mix(row0, nrows):
        P.dma("stmix", lambda e: e.dma_start(out=mixd[row0:row0 + nrows, :], in_=OT[0:nrows, :]), r=["OT"], w=[("mixd", row0)])

    def banded_unit(l, kind, h, g, ucol, need_kv, first, last):
        if kind == "A":
            d, koff, nblk, ntile = 1, 0, 32, 3
            bsrc = biasA_in[h]
        else:
            d = D_PAT[g][1]
            koff, nblk, ntile = -PADK, 33, 6
            bsrc = biasD_in[h * 3 + g]
        bi = cnt["uid"] % 2
        cnt["uid"] += 1
        P.dma("ldb%d" % bi, lambda e: e.dma_start(out=biasT[bi][:, 0:ntile, :], in_=bsrc), w=[("biasT", bi)])
        P.pool(lambda e: e.tensor_copy(out=biasb[bi][:, 0:ntile, :], in_=biasT[bi][:, 0:ntile, :]), r=[("biasT", bi)], w=[("biasb", bi)])
        wi = load_wu(l, ucol, 192)
        for j in range(8):
            def evq(ps, tag, j=j):
                P.act(lambda e: e.activation(out=perm_dst(QT[64:128, :], 0, d, j), in_=perm_src(ps[0:64, :], d), func=AF.Copy, scale=0.125),
                      r=[tag], w=["QT"])
            proj_fm(wi, 0, 64, j, evq)
            if need_kv:
                def evkv(ps, tag, j=j):
                    P.dve(lambda e: e.tensor_copy(out=perm_dst(KT[:, :], PADK, d, j), in_=perm_src(ps[:, :], d)),
                          r=[tag], w=["KT"])
                proj_fm(wi, 64, 128, j, evkv)
        if need_kv:
            pTv = pbT[:]
            for g8 in range(0, nblk, 8):
                nb_ = min(8, nblk - g8)
                for si in range(nb_):
                    kc = PADK + koff + 128 * (g8 + si)
                    P.pe(lambda e, si=si, kc=kc: e.transpose(out=pTv[:, si, 0:64], in_=KT[0:64, kc:kc + 128], identity=identb[0:64, 0:64]),
                         r=["KT", "identb"], w=[("pb", 7)])
                vt = []
                for si in range(nb_):
                    vt += vtags(g8 + si)
                P.act(lambda e, g8=g8, nb_=nb_: e.copy(out=VV[:, g8:g8 + nb_, 0:64], in_=pTv[:, 0:nb_, 0:64]), r=[("pb", 7), "VVi"], w=vt)
        bpr = 32 // d

        def kbs_of(b):
            if kind == "A":
                return [(b - 1 + i, i) for i in range(3) if 0 <= b - 1 + i < 32]
            if b % bpr == 0:
                t0_ = 2
            elif b % bpr == bpr - 1:
                t0_ = 4
            else:
                t0_ = 0
            return [(b, t0_), (b + 1, t0_ + 1)]

        def stage_a(b):
            kbs = kbs_of(b)
            nk = len(kbs)
            sb_ = (4, 5, 0, 3)[b % 4]
            sps = pb[sb_][:, 0:384].rearrange("p (a b) -> p a b", a=3)
            for i, (m, ti) in enumerate(kbs):
                kc = PADK + koff + 128 * m
                P.pe(lambda e, i=i, kc=kc, sps=sps, b=b: e.matmul(sps[:, i, :], lhsT=KT[64:128, kc:kc + 128], rhs=QT[64:128, b * 128:(b + 1) * 128],
                                                                  start=True, stop=False), r=["QT", "KT"], w=[("pb", sb_)])
                P.pe(lambda e, i=i, ti=ti, sps=sps: e.matmul(sps[:, i, :], lhsT=identb[:], rhs=biasb[bi][:, ti, :], start=False, stop=True),
                     r=["identb", ("biasb", bi)], w=[("pb", sb_)])
            pt = (PT[0], PT[1], PT[2], sqb[2])[b % 4]
            ptag = (("PT", 0), ("PT", 1), ("PT", 2), ("sqb", 2))[b % 4]
            ptv = pt[:, 0:nk * 128].rearrange("p (a b) -> p a b", a=nk)
            P.act(lambda e, sps=sps, ptv=ptv, nk=nk: e.activation(out=ptv, in_=sps[:, 0:nk, :], func=AF.Exp),
                  r=[("pb", sb_)], w=[ptag])

        def stage_b(b):
            kbs = kbs_of(b)
            nk = len(kbs)
            pt = (PT[0], PT[1], PT[2], sqb[2])[b % 4]
            ptag = (("PT", 0), ("PT", 1), ("PT", 2), ("sqb", 2))[b % 4]
            ob = (6, 1)[b % 2]
            ops = pb[ob][:]
            for i, (m, ti) in enumerate(kbs):
                P.pe(lambda e, i=i, m=m, pt=pt, ops=ops, nk=nk: e.matmul(ops[:, 0:128], lhsT=VV[:, m, :], rhs=pt[:, i * 128:(i + 1) * 128],
                                                                       start=(i == 0), stop=(i == nk - 1)),
                     r=vtags(m) + [ptag], w=[("pb", ob)])
            if kind == "A":
                finalize_block(ops, ("pb", ob), 128, OT[0:64, b * 128:(b + 1) * 128], esink[64:128, h:h + 1])
            else:
                Ld = S // d
                r_, i0 = divmod(128 * b, Ld)
                c0 = r_ + d * i0
                asl = acc[:, c0:c0 + 127 * d + 1:d]
                if first:
                    P.dve(lambda e, asl=asl, ops=ops: e.tensor_copy(out=asl, in_=ops[:, 0:128]), r=[("pb", ob)], w=["acc"])
                else:
                    P.dve(lambda e, asl=asl, ops=ops: e.tensor_tensor(out=asl, in0=ops[:, 0:128], in1=asl, op=ALU.add),
                          r=[("pb", ob), "acc"], w=["acc"])

        LAB = 3
        for b in range(LAB):
            stage_a(b)
        for b in range(32):
            if b + LAB < 32:
                stage_a(b + LAB)
            stage_b(b)
        if kind == "A":
            store_mix(h * 64, 64)
        elif last:
            for q4 in range(4):
                sl = slice(q4 * 1024, (q4 + 1) * 1024)
                P.act(lambda e, sl=sl: e.copy(out=etmp[0:64, :], in_=acc[64:128, sl]), r=["acc"], w=[("etmp", 0), ("etmp", 1)])
                P.dve(lambda e: e.reciprocal(out=etmp[0:64, :], in_=etmp[0:64, :]), r=[("etmp", 0)], w=[("etmp", 0), ("etmp", 1)])
                P.dve(lambda e, sl=sl: e.tensor_tensor(out=OT[0:64, sl], in0=acc[0:64, sl], in1=etmp[0:64, :], op=ALU.mult),
                      r=["acc", ("etmp", 0)], w=["OT"])
            store_mix(768 + h * 64, 64)

    def latent_unit(l):
        wi = load_wu(l, UCL, 384)
        for j in range(8):
            tsl = slice(j * 512, (j + 1) * 512)
            pss = []
            for ci, (c0, bank) in enumerate(((0, 0), (128, 1), (256, 3))):
                ps = pb[bank][:]
                for k in range(8):
                    P.pe(lambda e, k=k, ps=ps, c0=c0, tsl=tsl, wi=wi: e.matmul(ps, lhsT=wu[wi][:, k, c0:c0 + 128], rhs=hT[:, k, tsl],
                                                               start=(k == 0), stop=(k == 7)), r=[("wu", wi)], w=[("pb", bank)])
                P.act(lambda e, ps=ps, ci=ci: e.activation(out=sqb[ci][:], in_=ps, func=AF.Square), r=[("pb", bank)], w=[("sqb", ci)])
                pss.append((ps, ("pb", bank)))
            sq, sk = pb[4][:], pb[5][:]
            P.pe(lambda e: e.matmul(sq, lhsT=onesb[:], rhs=sqb[0][:], start=True, stop=False), r=[("sqb", 0), "onesb"], w=[("pb", 4)])
            P.pe(lambda e: e.matmul(sq, lhsT=onesb[:], rhs=sqb[1][:], start=False, stop=True), r=[("sqb", 1), "onesb"], w=[("pb", 4)])
            P.pe(lambda e: e.matmul(sk, lhsT=onesb[:], rhs=sqb[2][:], start=True, stop=True), r=[("sqb", 2), "onesb"], w=[("pb", 5)])
            for (srcp, stag, n_inv, dstt, tag) in ((sq, ("pb", 4), 1.0 / 256, rt1, "rt1"), (sk, ("pb", 5), 1.0 / 128, rt2, "rt2")):
                P.act(lambda e, srcp=srcp, n_inv=n_inv, dstt=dstt: e.activation(out=dstt[:], in_=srcp, func=AF.Sqrt, scale=n_inv, bias=epsb[:, 0:1]),
                      r=[stag, "epsb"], w=[tag])
                P.dve(lambda e, dstt=dstt: e.reciprocal(out=dstt[:], in_=dstt[:]), r=[tag], w=[tag])
            for ci in range(2):
                P.dve(lambda e, ci=ci, tsl=tsl, pss=pss: e.scalar_tensor_tensor(out=cqnT[:, ci, tsl], in0=pss[ci][0], scalar=gq[:, ci:ci + 1], in1=rt1[:],
                                                              op0=ALU.mult, op1=ALU.mult), r=[pss[ci][1], "rt1", "gq"], w=["cqnT"])
            P.dve(lambda e, tsl=tsl, pss=pss: e.scalar_tensor_tensor(out=ckvnT[:, tsl], in0=pss[2][0], scalar=gkv[:, 0:1], in1=rt2[:],
                                                   op0=ALU.mult, op1=ALU.mult), r=[pss[2][1], "rt2", "gq"], w=["ckvnT"])
        wi = load_wu(l, UCL + 384, 64)
        for j in range(8):
            tsl = slice(j * 512, (j + 1) * 512)
            for ab, bank in ((0, 6), (1, 0)):
                ps = pb[bank][:]
                for k in range(8):
                    P.pe(lambda e, k=k, ps=ps, ab=ab, tsl=tsl, wi=wi: e.matmul(ps[0:32, :], lhsT=wu[wi][:, k, ab * 32:(ab + 1) * 32], rhs=hT[:, k, tsl],
                                                               start=(k == 0), stop=(k == 7)), r=[("wu", wi)], w=[("pb", bank)])
            P.dve(lambda e, tsl=tsl: e.tensor_tensor(out=rt1[0:32, :], in0=pb[6][0:32, :], in1=TC[0:32, tsl], op=ALU.mult),
                  r=[("pb", 6), "TC"], w=["rt1"])
            P.dve(lambda e, tsl=tsl: e.tensor_tensor(out=rt2[0:32, :], in0=pb[0][0:32, :], in1=TS[0:32, tsl], op=ALU.mult),
                  r=[("pb", 0), "TS"], w=["rt2"])
            P.dve(lambda e, j=j: e.tensor_tensor(out=KT[64:96, PADK + j * 512:PADK + (j + 1) * 512], in0=rt1[0:32, :], in1=rt2[0:32, :],
                                                 op=ALU.add), r=["rt1", "rt2"], w=["KT"])

    def mla_unit(l, h):
        scale = 96.0 ** -0.5
        stage_cast("wq", wq_in[l][:, h * 192:(h + 1) * 192].rearrange("(k p) n -> p k n", p=128), (2, 192), wqh[:], ["wqh"])
        for j in range(8):
            tsl = slice(j * 512, (j + 1) * 512)
            pA, pB, pK = pb[0][:], pb[1][:], pb[3][:]
            for ab, ps, bank in ((0, pA, 0), (1, pB, 1)):
                for ci in range(2):
                    P.pe(lambda e, ps=ps, ab=ab, ci=ci, tsl=tsl: e.matmul(ps[0:96, :], lhsT=wqh[:, ci, ab * 96:(ab + 1) * 96], rhs=cqnT[:, ci, tsl],
                                                                          start=(ci == 0), stop=(ci == 1)), r=["wqh", "cqnT"], w=[("pb", bank)])
            P.pe(lambda e, tsl=tsl: e.matmul(pK[0:64, :], lhsT=wkv[:, h * 128:h * 128 + 64], rhs=ckvnT[:, tsl], start=True, stop=True),
                 r=["wkv", "ckvnT"], w=[("pb", 3)])
            P.act(lambda e, tsl=tsl: e.activation(out=QT[0:64, tsl], in_=pA[0:64, :], func=AF.Copy, scale=scale), r=[("pb", 0)], w=["QT"])
            P.dve(lambda e, tsl=tsl: e.tensor_tensor(out=rt1[64:96, :], in0=pA[64:96, :], in1=TC[64:96, tsl], op=ALU.mult),
                  r=[("pb", 0), "TC"], w=["rt1"])
            P.dve(lambda e, tsl=tsl: e.tensor_tensor(out=rt2[64:96, :], in0=pB[64:96, :], in1=TS[64:96, tsl], op=ALU.mult),
                  r=[("pb", 1), "TS"], w=["rt2"])
            P.dve(lambda e: e.tensor_tensor(out=rt1[64:96, :], in0=rt1[64:96, :], in1=rt2[64:96, :], op=ALU.add),
                  r=["rt1", "rt2"], w=["rt1"])
            P.act(lambda e, tsl=tsl: e.activation(out=QT[64:96, tsl], in_=rt1[64:96, :], func=AF.Copy, scale=scale), r=["rt1"], w=["QT"])
            P.act(lambda e, j=j: e.copy(out=KT[0:64, PADK + j * 512:PADK + (j + 1) * 512], in_=pK[0:64, :]), r=[("pb", 3)], w=["KT"])
        import os as _os
        _mv = int(_os.environ.get("MV", "9"))
        if _mv < 2:
            return
        for g4 in range(4):
            gi = vslot[0]
            vslot[0] += 1
            bank = (2, 4)[gi % 2]
            ps = pb[bank][:]
            for si in range(8):
                m = g4 * 8 + si
                P.pe(lambda e, m=m, si=si, ps=ps: e.matmul(ps[:, si * 64:(si + 1) * 64], lhsT=ckvnT[:, m * 128:(m + 1) * 128],
                                                         rhs=wkv[:, h * 128 + 64:h * 128 + 128], start=True, stop=True),
                     r=["wkv", "ckvnT"], w=[("pb", bank)])
            vt = []
            for si in range(8):
                vt += vtags(g4 * 8 + si)
            psv_ = ps.rearrange("p (a b) -> p a b", a=8)
            if gi % 2 == 0:
                P.act(lambda e, g4=g4, psv_=psv_: e.copy(out=VV[:, g4 * 8:(g4 + 1) * 8, 0:64], in_=psv_), r=[("pb", bank), "VVi"], w=vt)
            else:
                P.dve(lambda e, g4=g4, psv_=psv_: e.tensor_copy(out=VV[:, g4 * 8:(g4 + 1) * 8, 0:64], in_=psv_), r=[("pb", bank), "VVi"], w=vt)
        if _mv < 3:
            return
        items = [(qt, m) for qt in range(8) for m in range(32)]

        def mla_a(it):
            qt, m = items[it]
            qsl = slice(qt * 512, (qt + 1) * 512)
            sbk = (3, 4, 5)[it % 3]
            pti = it % 3
            sps = pb[sbk][:]
            pt = PT[pti]
            kc = PADK + 128 * m
            P.pe(lambda e, sps=sps, kc=kc, qsl=qsl: e.matmul(sps, lhsT=KT[0:96, kc:kc + 128], rhs=QT[0:96, qsl], start=True, stop=True),
                 r=["QT", "KT"], w=[("pb", sbk)])
            P.act(lambda e, sps=sps, pt=pt: e.activation(out=pt[:], in_=sps, func=AF.Exp), r=[("pb", sbk)], w=[("PT", pti)])

        def mla_b(it):
            qt, m = items[it]
            qsl = slice(qt * 512, (qt + 1) * 512)
            pti = it % 3
            pt = PT[pti]
            ob = (6, 1)[qt % 2]
            ops = pb[ob][:]
            P.pe(lambda e, m=m, pt=pt, ops=ops: e.matmul(ops, lhsT=VV[:, m, :], rhs=pt[:], start=(m == 0), stop=(m == 31)),
                 r=vtags(m) + [("PT", pti)], w=[("pb", ob)])
            if m == 31 and _mv >= 4:
                finalize_block(ops, ("pb", ob), 512, OT[0:64, qsl], None)

        LA = 2
        for it in range(min(LA, len(items))):
            mla_a(it)
        for it in range(len(items)):
            if it + LA < len(items):
                mla_a(it + LA)
            mla_b(it)
        if _mv >= 5:
            store_mix(512 + h * 64, 64)

    def b_unit(l, ch):
        wi = load_wu(l, UB + ch * 384, 384)
        ZC = PADK
        for j in range(8):
            tsl = slice(j * 512, (j + 1) * 512)
            for which, bank in ((1, 0), (2, 1), (0, 3)):
                ps = pb[bank][:]
                for k in range(8):
                    P.pe(lambda e, k=k, ps=ps, which=which, tsl=tsl: e.matmul(ps, lhsT=wu[wi][:, k, which * 128:(which + 1) * 128], rhs=hT[:, k, tsl],
                                                                              start=(k == 0), stop=(k == 7)), r=[("wu", wi)], w=[("pb", bank)])
            P.act(lambda e: e.copy(out=sqb[0][:], in_=pb[0][:]), r=[("pb", 0)], w=[("sqb", 0)])
            P.dve(lambda e, j=j: e.tensor_tensor(out=KT[:, ZC + j * 512:ZC + (j + 1) * 512], in0=pb[1][:], in1=sqb[0][:], op=ALU.mult),
                  r=[("pb", 1), ("sqb", 0)], w=["KT"])
            P.act(lambda e, tsl=tsl: e.copy(out=QT[:, tsl], in_=pb[3][:]), r=[("pb", 3)], w=["QT"])
        for q4 in range(4):
            sl = slice(q4 * 1024, (q4 + 1) * 1024)
            zc = ZC + q4 * 1024
            P.act(lambda e, sl=sl, zc=zc: e.activation(out=acc[:, sl], in_=KT[:, zc:zc + 1024], func=AF.Identity, scale=bconv[:, ch, 1:2]),
                  r=["KT", "bconv", "acc"], w=["acc"])
            P.dve(lambda e, sl=sl, zc=zc: e.scalar_tensor_tensor(out=acc[:, sl], in0=KT[:, zc - 1:zc + 1023], scalar=bconv[:, ch, 0:1],
                                                                  in1=acc[:, sl], op0=ALU.mult, op1=ALU.add), r=["KT", "bconv", "acc"], w=["acc"])
            P.dve(lambda e, sl=sl, zc=zc: e.scalar_tensor_tensor(out=acc[:, sl], in0=KT[:, zc + 1:zc + 1025], scalar=bconv[:, ch, 2:3],
                                                                  in1=acc[:, sl], op0=ALU.mult, op1=ALU.add), r=["KT", "bconv", "acc"], w=["acc"])
            P.pool(lambda e, sl=sl: e.tensor_tensor(out=OT[:, sl], in0=acc[:, sl], in1=QT[:, sl], op=ALU.mult), r=["acc", "QT"], w=["OT"])
        store_mix(256 + ch * 128, 128)

    for l in range(nlayers):
        src = x_in if l == 0 else out
        P.phase = "S0"
        P.dma("ld_row", lambda e, l=l: e.dma_start(out=rowv[0:1, 0:6 * DM], in_=b_mod[l]), w=["rowv"])
        P.dma("ld_row", lambda e, l=l: e.dma_start(out=rowv[0:1, 6 * DM:10 * DM], in_=norm_g[l]), r=["rowv"], w=["rowv"])
        P.dma("ld_sm0", lambda e, l=l: e.dma_start(out=esink[:], in_=sink_in[l]), w=["esink"])
        P.dma("ld_sm1", lambda e, l=l: e.dma_start(out=bconv[:], in_=bconv_in[l]), w=["bconv"])
        P.dma("ld_sm2", lambda e, l=l: e.dma_start(out=gq[:], in_=gq_in[l]), w=["gq"])
        P.dma("ld_sm3", lambda e, l=l: e.dma_start(out=gkv[:], in_=gkv_in[l]), r=["gq"], w=["gq"])
        P.dma("ld_sm4", lambda e, l=l: e.dma_start(out=fconv[:], in_=fconv_in[l]), w=["fconv"])
        stage_cast("wkv", wkv_in[l], (1, 512), wkv[:], ["wkv"])
        P.act(lambda e: e.activation(out=esink[:], in_=esink[:], func=AF.Exp), r=["esink"], w=["esink"])
        P.pool(lambda e: e.memset(cactB[:], 1.0), w=["cactB"])
        for k in range(8):
            P.dve(lambda e, k=k: e.tensor_scalar(out=cactB[:, k, :], in0=cactB[:, k, :], scalar1=cact[:, 8 + k:9 + k],
                                                 scalar2=None, op0=ALU.mult), r=["cact", "cactB"], w=["cactB"])
        for n in range(12 if "NOMOD" not in PH else 0):
            bi = n % 2
            P.dma("ldwm%d" % bi, lambda e, l=l, n=n, bi=bi: e.dma_start(
                out=wm[bi][:], in_=w_mod[l][:, n * 512:(n + 1) * 512].rearrange("(k p) n -> p k n", p=128)), w=[("wm", bi)])
            ps = pb[bi][:]
            for k in range(8):
                P.pe(lambda e, k=k, ps=ps, bi=bi: e.matmul(ps, lhsT=cactB[:, k, :], rhs=wm[bi][:, k, :], start=(k == 0), stop=False),
                     r=[("wm", bi), "cactB"], w=[("pb", bi)])
            P.pe(lambda e, ps=ps, n=n: e.matmul(ps, lhsT=ones1[0:1, :], rhs=rowv[0:1, n * 512:(n + 1) * 512], start=False, stop=True),
                 r=["rowv", "ones1"], w=[("pb", bi)])
            if n % 2 == 0:
                P.act(lambda e, ps=ps, n=n: e.copy(out=modB[:, n * 512:(n + 1) * 512], in_=ps), r=[("pb", bi)], w=[("modB", n)])
            else:
                P.dve(lambda e, ps=ps, n=n: e.tensor_copy(out=modB[:, n * 512:(n + 1) * 512], in_=ps), r=[("pb", bi)], w=[("modB", n)])

        GBK = (3, 5)

        def gain_bcast(j):
            for hf in range(2):
                P.pe(lambda e, hf=hf: e.matmul(pb[GBK[hf]][:], lhsT=ones1[0:1, :],
                                               rhs=rowv[0:1, 6 * DM + j * DM + hf * 512:6 * DM + j * DM + (hf + 1) * 512],
                                               start=True, stop=True), r=["rowv", "ones1"], w=[("pb", GBK[hf])])

        def make_T(src_ap_fn, dstT, rtags):
            pT = pb[4][:].rearrange("p (a b) -> p a b", a=4)
            for half in range(2):
                for k4 in range(4):
                    k = half * 4 + k4
                    P.pe(lambda e, k=k, k4=k4: e.transpose(out=pT[:, k4, :], in_=src_ap_fn(k), identity=identf[:]),
                         r=list(rtags) + ["identf"], w=[("pb", 4)])
                P.dve(lambda e, half=half: e.tensor_copy(out=dstT[:, half * 4:(half + 1) * 4], in_=pT[:, :, 0]), r=[("pb", 4)], w=["modT"])

        def mtag(off):
            return [("modB", off // 512), ("modB", off // 512 + 1)]

        for (jn, sc_off, sh_off, g_off, jg, sT_, shT_, GB_) in () if "NODER" in PH else ((0, 1 * DM, 0, 2 * DM, 1, s1T, sh1T, G1B), (2, 4 * DM, 3 * DM, 5 * DM, 3, s2T, sh2T, G2B)):
            gain_bcast(jn)
            for hf in range(2):
                sl = slice(hf * 512, (hf + 1) * 512)
                P.dve(lambda e, hf=hf, sl=sl, sc_off=sc_off: e.scalar_tensor_tensor(
                    out=tmpA[:, sl], in0=modB[:, sc_off + hf * 512:sc_off + (hf + 1) * 512], scalar=1.0, in1=pb[GBK[hf]][:],
                    op0=ALU.add, op1=ALU.mult), r=mtag(sc_off) + [("pb", GBK[hf])], w=["tmpA"])
            make_T(lambda k: tmpA[:, k * 128:(k + 1) * 128], sT_, ["tmpA"])
            make_T(lambda k, sh_off=sh_off: modB[:, sh_off + k * 128:sh_off + (k + 1) * 128], shT_, mtag(sh_off))
            gain_bcast(jg)
            for hf in range(2):
                sl = slice(hf * 512, (hf + 1) * 512)
                P.dve(lambda e, hf=hf, sl=sl, g_off=g_off, GB_=GB_: e.tensor_tensor(
                    out=GB_[:, sl], in0=modB[:, g_off + hf * 512:g_off + (hf + 1) * 512], in1=pb[GBK[hf]][:], op=ALU.mult),
                    r=mtag(g_off) + [("pb", GBK[hf])], w=["GB"])
        P.barrier()
        build_rope()
        P.pool(lambda e: e.memset(KT[:], 0.0), w=["KT"])
        P.pool(lambda e: e.memset(VV[:], 0.0), w=["VVi"])
        P.pool(lambda e: e.memset(VV[:, :, 64:128], 1.0), r=["VVi"], w=["VVi"])

        P.phase = "P1"
        if "P1" in PH:
            norm_blocks(src, list(range(32)), s1T, sh1T, hT, lambda j, bb: bb * 128, lambda j, bb, e_: ("hT", bb, e_))
        P.barrier()

        if debug and "HT" in PH and l == 0:
            for k in range(8):
                P.dma("dbg", lambda e, k=k: e.dma_start(out=dbg[k * 128:(k + 1) * 128, :], in_=hT[:, k, :]), w=[("dbgd", k)])
        P.phase = "LAT"
        if "LAT" in PH:
            latent_unit(l)
        P.phase = "MLA"
        if "MLA" in PH:
            for h in range(4):
                mla_unit(l, h)
                if debug and "DMLA" in PH and l == 0 and h == 0:
                    P.barrier()
                    P.dma("dbg", lambda e: e.dma_start(out=dbg[0:128, :], in_=QT[:, :]), w=[("dbgd", 0)])
                    P.dma("dbg", lambda e: e.dma_start(out=dbg[128:256, :], in_=KT[:, PADK:PADK + S]), w=[("dbgd", 1)])
                    P.dma("dbg", lambda e: e.dma_start(out=dbg[256:384, :], in_=cqnT[:, 0, :]), w=[("dbgd", 2)])
                    P.dma("dbg", lambda e: e.dma_start(out=dbg[384:512, :], in_=cqnT[:, 1, :]), w=[("dbgd", 3)])
                    P.dma("dbg", lambda e: e.dma_start(out=dbg[512:640, :], in_=ckvnT[:, :]), w=[("dbgd", 4)])
                    P.dma("dbg", lambda e: e.dma_start(out=dbg[640:768, :], in_=VV[:, 0:32, :].rearrange("p a b -> p (a b)")), w=[("dbgd", 5)])
                    P.dma("dbg", lambda e: e.dma_start(out=dbg[768:896, 0:512], in_=rt1[:, :]), w=[("dbgd", 6)], eng="pool")
                    P.dma("dbg", lambda e: e.dma_start(out=dbg[768:896, 512:1024], in_=sqb[1][:, :]), w=[("dbgd", 8)])
                    P.dma("dbg", lambda e: e.dma_start(out=dbg[768:896, 1024:1536], in_=rt2[:, :]), w=[("dbgd", 9)], eng="pool")
                    P.dma("dbg", lambda e: e.dma_start(out=dbg[768:896, 1536:1538], in_=gq[:, :], allow_slow_non_contiguous=True), w=[("dbgd", 10)], eng="pool")
                    P.dma("dbg", lambda e: e.dma_start(out=dbg[768:896, 1538:1539], in_=gkv[:, :], allow_slow_non_contiguous=True), w=[("dbgd", 11)], eng="pool")
                    P.dma("dbg", lambda e: e.dma_start(out=dbg[768:896, 1540:1541], in_=epsb[:, :], allow_slow_non_contiguous=True), w=[("dbgd", 12)], eng="pool")
                    P.dma("dbg", lambda e: e.dma_start(out=dbg[896:1024, :], in_=TC[:, :]), w=[("dbgd", 7)])
                    P.barrier()
                    break
        P.barrier()
        P.phase = "A"
        if "A" in PH:
            for h in range(4):
                banded_unit(l, "A", h, 0, UA + h * 192, need_kv=(h % 2 == 0), first=True, last=True)
        P.phase = "D"
        if "D" in PH:
            for h in range(4):
                for g in range(3):
                    banded_unit(l, "D", h, g, UD + (h * 3 + g) * 192, need_kv=True, first=(g == 0), last=(g == 2))
        P.phase = "B"
        if "B" in PH:
            for ch in range(2):
                b_unit(l, ch)
        P.barrier()
        if "P3" not in PH:
            continue

        P.phase = "P3"
        for q4 in range(4):
            sv = acc[:, (q4 % 2) * 2048:(q4 % 2 + 1) * 2048].rearrange("p (a b) -> p a b", a=2)
            P.dma("ldwo%d" % (q4 % 2), lambda e, l=l, q4=q4, sv=sv: e.dma_start(
                out=sv, in_=w_out[l][q4 * 256:(q4 + 1) * 256, :].rearrange("(k p) n -> p k n", p=128)), w=[("accst", q4 % 2)])
            if q4 % 2 == 0:
                P.act(lambda e, q4=q4, sv=sv: e.copy(out=wo[:, q4 * 2:(q4 + 1) * 2, :], in_=sv), r=[("accst", q4 % 2)], w=[("wo", 0)])
            else:
                P.pool(lambda e, q4=q4, sv=sv: e.tensor_copy(out=wo[:, q4 * 2:(q4 + 1) * 2, :], in_=sv), r=[("accst", q4 % 2)], w=[("wo", 0)])
        if debug and l == 0 and "HT" not in PH:
            for k in range(8):
                P.dma("dbg", lambda e, k=k: e.dma_start(out=mixl[0][:, :, 0:512], in_=mixd[:, k * 512:(k + 1) * 512].rearrange("(c p) t -> p c t", p=128)),
                      r=[("mixl", 0)], w=[("mixl", 0)])
                P.dma("dbg", lambda e, k=k: e.dma_start(out=dbg[:, k * 512:(k + 1) * 512].rearrange("(c p) t -> p c t", p=128), in_=mixl[0][:, :, 0:512]),
                      r=[("mixl", 0)], w=[("mixl", 0)])
        def ldmix(b4):
            mi_ = b4 % 2
            P.dma("ldmix%d" % mi_, lambda e: e.dma_start(out=mixl[mi_][:], in_=mixd[:, b4 * 512:(b4 + 1) * 512].rearrange("(k p) t -> p k t", p=128)),
                  w=[("mixl", mi_)])
        ldmix(0)
        for b4 in range(8):
            mi = b4 % 2
            if b4 + 1 < 8:
                ldmix(b4 + 1)
            for bb in range(4):
                b = b4 * 4 + bb
                bk = (0, 1) if b % 2 == 0 else (3, 4)
                for hf in range(2):
                    ps = pb[bk[hf]][:]
                    for k in range(8):
                        P.pe(lambda e, k=k, ps=ps, hf=hf, mi=mi, bb=bb: e.matmul(ps, lhsT=mixl[mi][:, k, bb * 128:(bb + 1) * 128],
                                                                                  rhs=wo[:, k, hf * 512:(hf + 1) * 512],
                                                                                  start=(k == 0), stop=(k == 7)),
                             r=[("mixl", mi), ("wo", 0)], w=[("pb", bk[hf])])
                epilogue(pb[bk[0]][:], pb[bk[1]][:], (("pb", bk[0]), ("pb", bk[1])), src, xmid, b * 128, 128, G1B)
        P.barrier()

        if "P4" not in PH:
            continue
        P.phase = "P4"
        fcnt = [0]

        def fstage(src_ap, a_, b_, dst_ap, dtags, eng="pool"):
            i = fcnt[0] % 2
            fcnt[0] += 1
            sv = fst[i][:, 0:a_ * b_].rearrange("p (a b) -> p a b", a=a_)
            P.dma("ldf%d" % i, lambda e: e.dma_start(out=sv, in_=src_ap), w=[("fst", i)])
            if eng == "pool":
                P.pool(lambda e: e.tensor_copy(out=dst_ap, in_=sv), r=[("fst", i)], w=dtags)
            else:
                P.act(lambda e: e.copy(out=dst_ap, in_=sv), r=[("fst", i)], w=dtags)

        def fdma(src_ap, a_, b_):
            i = fcnt[0] % 2
            fcnt[0] += 1
            sv = fst[i][:, 0:a_ * b_].rearrange("p (a b) -> p a b", a=a_)
            P.dma("ldf%d" % i, lambda e: e.dma_start(out=sv, in_=src_ap), w=[("fst", i)])
            return i, sv

        def fcast(i, sv, dst_ap, dtags):
            P.act(lambda e: e.copy(out=dst_ap, in_=sv), r=[("fst", i)], w=dtags)

        items = [(gi_, cb_) for gi_ in range(4) for cb_ in range(6)]
        pend = {}

        def wstep(it, step):
            if it >= len(items):
                return
            gi_, cb_ = items[it]
            wi_ = it % 2
            ncols = min(512, DFF - cb_ * 512)

            def src(cbase, kh):
                return w_up[l][kh * 512:(kh + 1) * 512, cbase + cb_ * 512:cbase + cb_ * 512 + ncols].rearrange("(k p) n -> p k n", p=128)

            def dst(gv, kh):
                return wup[wi_][:, kh * 4:(kh + 1) * 4, gv * 512:gv * 512 + ncols]
            if step == 0:
                pend[(it, 0)] = [fdma(src(0, kh), 4, ncols) for kh in range(2)]
            elif step == 1:
                for kh, (i_, sv_) in enumerate(pend.pop((it, 0))):
                    fcast(i_, sv_, dst(0, kh), [("wup", wi_, 0)])
                pend[(it, 1)] = [fdma(src(DFF, kh), 4, ncols) for kh in range(2)]
            else:
                for kh, (i_, sv_) in enumerate(pend.pop((it, 1))):
                    fcast(i_, sv_, dst(1, kh), [("wup", wi_, 1)])

        for q11 in range(11):
            fstage(w_down[l][q11 * 256:(q11 + 1) * 256, :].rearrange("(c p) n -> p c n", p=128), 2, 1024,
                   wdn[:, q11 * 2:(q11 + 1) * 2, :], [("wdn", 0)], eng=("act" if q11 % 2 == 0 else "pool"))
        wdn_tags = [("wdn", 0)]
        P.pool(lambda e: e.memset(hT2[:, :, 0:1], 0.0), w=[("hT2", "z0")])
        P.pool(lambda e: e.memset(hT2[:, :, HT2C - 1:HT2C], 0.0), w=[("hT2", "z1")])
        groups = ((0, 1020), (1020, 2040), (2040, 3060), (3060, 4096))
        for st_ in range(3):
            wstep(0, st_)
        for gidx, (g0, g1) in enumerate(groups):
            blk0 = max(g0 - 1, 0) // 128
            blk1 = min((g1 + 1 + 127) // 128, 32)
            P.phase = "P4n"
            norm_blocks(xmid, list(range(blk0, blk1)), s2T, sh2T, hT2, lambda j, bb: 1 + j * 128, lambda j, bb, e_: ("hT2", j, e_))
            htags = [("hT2", "z0"), ("hT2", "z1")] + [("hT2", s_, e_) for s_ in range(blk1 - blk0) for e_ in (0, 1)]
            tiles = []
            o = g0
            while o < g1:
                n = min(510, g1 - o)
                tiles.append((o, n))
                o += n
            P.phase = "P4c"
            for c_ in range(22):
                cb4, c4 = divmod(c_, 4)
                it_ = gidx * 6 + cb4
                wi = it_ % 2
                nc4 = 4 if cb4 < 5 else 2
                if c4 == 0:
                    wstep(it_ + 1, 0)
                if c4 == 1:
                    wstep(it_ + 1, 1)
                if c4 == min(2, nc4 - 1):
                    wstep(it_ + 1, 2)
                for ti, (o0, n) in enumerate(tiles):
                    N = n + 2
                    cs = 1 + (o0 - 1) - blk0 * 128
                    par = ti % 2
                    bg, bv = (0, 1) if par == 0 else (3, 4)
                    psg, psv = pb[bg][:], pb[bv][:]
                    for (ps, wc, bank) in ((psg, c4 * 128, bg), (psv, 512 + c4 * 128, bv)):
                        for k in range(8):
                            P.pe(lambda e, k=k, ps=ps, wc=wc, cs=cs, N=N, wi=wi: e.matmul(ps[:, 0:N], lhsT=wup[wi][:, k, wc:wc + 128], rhs=hT2[:, k, cs:cs + N],
                                                                                        start=(k == 0), stop=(k == 7)),
                                 r=[("wup", wi, 0), ("wup", wi, 1)] + htags, w=[("pb", bank)])
                    for (ps, tt, fc, bank, nm) in ((psg, tg[par], c_, bg, "tg"), (psv, tv[par], 22 + c_, bv, "tv")):
                        P.act(lambda e, ps=ps, tt=tt, fc=fc, n=n: e.activation(out=tt[:, 0:n], in_=ps[:, 1:n + 1], func=AF.Identity, scale=fconv[:, fc, 1:2]),
                              r=[("pb", bank), "fconv"], w=[(nm, par)])
                        P.dve(lambda e, ps=ps, tt=tt, fc=fc, n=n: e.scalar_tensor_tensor(out=tt[:, 0:n], in0=ps[:, 0:n], scalar=fconv[:, fc, 0:1], in1=tt[:, 0:n],
                                                                                     op0=ALU.mult, op1=ALU.add), r=[("pb", bank), "fconv", (nm, par)], w=[(nm, par)])
                        P.dve(lambda e, ps=ps, tt=tt, fc=fc, n=n: e.scalar_tensor_tensor(out=tt[:, 0:n], in0=ps[:, 2:n + 2], scalar=fconv[:, fc, 2:3], in1=tt[:, 0:n],
                                                                                     op0=ALU.mult, op1=ALU.add), r=[("pb", bank), "fconv", (nm, par)], w=[(nm, par)])
                    P.act(lambda e, par=par, n=n: e.activation(out=tgg[par][:, 0:n], in_=tg[par][:, 0:n], func=AF.Gelu_apprx_tanh),
                          r=[("tg", par)], w=[("tgg", par)])
                    P.pool(lambda e, par=par, n=n, c_=c_, o0=o0, g0=g0: e.tensor_tensor(out=GT[:, c_, o0 - g0:o0 - g0 + n], in0=tgg[par][:, 0:n], in1=tv[par][:, 0:n], op=ALU.mult),
                           r=[("tgg", par), ("tv", par)], w=[("GT", c_)])
            gtags = [("GT", c_) for c_ in range(22)]
            P.phase = "P4d"
            t = g0
            wdi = 0
            while t < g1:
                m = min(128, g1 - t)
                bk = (5, 6) if wdi % 2 == 0 else (2, 3)
                wdi += 1
                for hf in range(2):
                    ps = pb[bk[hf]][:]
                    for c_ in range(22):
                        P.pe(lambda e, c_=c_, ps=ps, hf=hf, t=t, m=m, g0=g0: e.matmul(ps[0:m, :], lhsT=GT[:, c_, t - g0:t - g0 + m], rhs=wdn[:, c_, hf * 512:(hf + 1) * 512],
                                                                             start=(c_ == 0), stop=(c_ == 21)), r=gtags + wdn_tags, w=[("pb", bk[hf])])
                epilogue(pb[bk[0]][:], pb[bk[1]][:], (("pb", bk[0]), ("pb", bk[1])), xmid, out, t, m, G2B)
                t += m
        P.barrier()

    P.emit(nc, final_chans=("stx0", "stx1", "dbg"))
    return nc, P


_CACHE = {}


def _prep_shared(inputs):
    f = lambda a: np.ascontiguousarray(np.asarray(a, dtype=np.float32))
    w_in = f(inputs["w_in"])
    cols = _wu_cols()
    wu = np.ascontiguousarray(w_in[:, :, cols])
    sink = np.ascontiguousarray(np.broadcast_to(f(inputs["a_sink"])[:, None, :], (NL, 128, 4)))
    bconv = np.ascontiguousarray(f(inputs["b_conv"]).reshape(NL, 3, 2, 128).transpose(0, 3, 2, 1))
    gq = np.ascontiguousarray(f(inputs["c_norm_q"]).reshape(NL, 2, 128).transpose(0, 2, 1))
    gkv = np.ascontiguousarray(f(inputs["c_norm_kv"]).reshape(NL, 128, 1))
    uq = f(inputs["c_w_uq"])
    wq = np.empty((NL, 256, 4, 192), np.float32)
    for h in range(4):
        b0 = h * 96
        wq[:, :, h, 0:96] = uq[:, :, b0:b0 + 96]
        wq[:, :, h, 96:160] = uq[:, :, b0:b0 + 64]
        wq[:, :, h, 160:176] = uq[:, :, b0 + 80:b0 + 96]
        wq[:, :, h, 176:192] = uq[:, :, b0 + 64:b0 + 80]
    wq = np.ascontiguousarray(wq.reshape(NL, 256, 768))
    fconv = np.ascontiguousarray(f(inputs["ffn_conv"]).reshape(NL, 3, 44, 128).transpose(0, 3, 2, 1))
    bA, bD = _bias_tables(inputs["rel_bias"])
    half = 16
    inv_freq = (10000.0 ** (-np.arange(half, dtype=np.float32) / half)).astype(np.float32)
    cst = np.zeros((128, 4), np.float32)
    for p in list(range(0, 32)) + list(range(64, 96)):
        cst[p, 0] = inv_freq[(p % 32) % 16]
        cst[p, 1] = -1.0 if (p % 32) < 16 else 1.0
    return dict(
        cst=cst, idn=np.eye(128, dtype=np.float32),
        w_mod=f(inputs["w_mod"]), b_mod=f(inputs["b_mod"]).reshape(NL, 1, 6 * DM),
        norm_g=f(inputs["norm_g"]).reshape(NL, 1, 4 * DM), wu=wu, sink=sink, bconv=bconv, gq=gq, gkv=gkv, wq=wq,
        wkv=f(inputs["c_w_ukv"]), w_out=f(inputs["w_out"]), w_up=f(inputs["w_up"]), fconv=fconv,
        w_down=f(inputs["w_down"]), biasA=bA, biasD=bD)


def _core_inputs(inputs, shared, b):
    m = dict(shared)
    m["x"] = np.ascontiguousarray(np.asarray(inputs["x"][b], dtype=np.float32))
    m["cT"] = np.ascontiguousarray(np.asarray(inputs["c"][b], dtype=np.float32).reshape(8, 128).T)
    m["pos"] = np.ascontiguousarray(np.asarray(inputs["positions"][b], dtype=np.int32).reshape(1, S))
    return m


def kernel(**inputs):
    if "nc" not in _CACHE:
        _CACHE["nc"] = build(NL, False)[0]
    nc = _CACHE["nc"]
    shared = _prep_shared(inputs)
    in_maps = [_core_inputs(inputs, shared, b) for b in range(8)]
    res = run_bass_kernel_spmd(nc, in_maps, core_ids=list(range(8)))
    return np.stack([np.asarray(r["out"], dtype=np.float32) for r in res.results], axis=0)
```

```python
import math
from contextlib import ExitStack

import numpy as np
import concourse.bass as bass
import concourse.mybir as mybir
from concourse.bass_utils import run_bass_kernel_spmd

F32 = mybir.dt.float32
BF16 = mybir.dt.bfloat16
I32 = mybir.dt.int32
ALU = mybir.AluOpType
AF = mybir.ActivationFunctionType

S = 4096
DM = 1024
NL = 4
DFF = 2816
EPS = 1e-6
NEGM = -30000.0
PADK = 64
SAME_ENGINE_SYNC = False
ENGS = ("pe", "act", "dve", "pool", "sp")


class Op:
    __slots__ = ("eng", "fn", "reads", "writes", "deps", "idx", "chan", "is_dma", "barrier", "phase")

    def __init__(self, eng, fn, reads, writes, chan=None, barrier=False):
        self.eng = eng
        self.fn = fn
        self.reads = reads
        self.writes = writes
        self.chan = chan
        self.is_dma = chan is not None
        self.barrier = barrier
        self.phase = ""


class Prog:
    def __init__(self):
        self.ops = []

    phase = ""

    def add(self, eng, fn, r=(), w=(), chan=None):
        op = Op(eng, fn, tuple(r), tuple(w), chan)
        op.phase = self.phase
        self.ops.append(op)

    def pe(self, fn, r=(), w=()):
        self.add("pe", fn, r, w)

    def act(self, fn, r=(), w=()):
        self.add("act", fn, r, w)

    def dve(self, fn, r=(), w=()):
        self.add("dve", fn, r, w)

    def pool(self, fn, r=(), w=()):
        self.add("pool", fn, r, w)

    def dma(self, chan, fn, r=(), w=(), eng="sp"):
        self.add(eng, fn, r, w, chan=chan)

    def barrier(self):
        for e in ENGS:
            self.ops.append(Op(e, None, (), (), None, barrier=True))

    def emit(self, nc, final_chans=()):
        ops = self.ops

        def stream(op):
            return ("dma", op.chan) if op.is_dma else op.eng

        last_writer, readers, last_in_stream = {}, {}, {}
        for i, op in enumerate(ops):
            op.idx = i
            if op.barrier:
                op.deps = set(last_in_stream.values())
                continue
            deps = set()
            for r in op.reads:
                lw = last_writer.get(r)
                if lw is not None:
                    deps.add(lw)
            for w in op.writes:
                lw = last_writer.get(w)
                if lw is not None:
                    deps.add(lw)
                deps.update(readers.get(w, ()))
            deps.discard(i)
            op.deps = deps
            for r in op.reads:
                readers.setdefault(r, []).append(i)
            for w in op.writes:
                last_writer[w] = i
                readers[w] = []
            last_in_stream[stream(op)] = i
        spos, cnt = {}, {}
        for op in ops:
            if op.barrier:
                continue
            s = stream(op)
            cnt[s] = cnt.get(s, 0) + 1
            spos[op.idx] = cnt[s]
        waited = {e: {} for e in ENGS}
        need = {}
        signaled = set()
        for op in ops:
            if op.is_dma:
                signaled.add(op.idx)
        for op in ops:
            e = op.eng
            best = {}
            for d in op.deps:
                dop = ops[d]
                s = stream(dop)
                if (not dop.is_dma) and dop.eng == e and not op.is_dma:
                    if e == "pe" or not SAME_ENGINE_SYNC or op.barrier:
                        continue
                if s not in best or spos[d] > spos[best[s]]:
                    best[s] = d
            lst = []
            for s, d in best.items():
                if waited[e].get(s, 0) >= spos[d]:
                    continue
                waited[e][s] = spos[d]
                lst.append((s, d))
                signaled.add(d)
            need[op.idx] = lst
        sigcount, sigval = {}, {}
        for op in ops:
            if op.idx in signaled:
                s = stream(op)
                sigcount[s] = sigcount.get(s, 0) + (16 if op.is_dma else 1)
                sigval[op.idx] = sigcount[s]
        streams = sorted(set(stream(ops[i]) for i in signaled), key=str)
        with ExitStack() as st:
            sems = {}
            for s in streams:
                nm = "s_" + (s if isinstance(s, str) else "d_" + str(s[1]))
                sems[s] = st.enter_context(nc.semaphore(nm))
            block = st.enter_context(nc.Block())
            per_eng = {e: [op for op in ops if op.eng == e] for e in ENGS}

            def run(e, eng):
                for op in per_eng[e]:
                    for (s, d) in need[op.idx]:
                        eng.wait_ge(sems[s], sigval[d])
                    if op.fn is None:
                        continue
                    ins = op.fn(eng)
                    if op.idx in sigval:
                        ins.then_inc(sems[stream(op)], 16 if op.is_dma else 1)
                if e == "sp":
                    for ch in final_chans:
                        s = ("dma", ch)
                        if s in sems:
                            eng.wait_ge(sems[s], sigcount[s])

            block.tensor(lambda eng: run("pe", eng))
            block.scalar(lambda eng: run("act", eng))
            block.vector(lambda eng: run("dve", eng))
            block.gpsimd(lambda eng: run("pool", eng))
            block.sync(lambda eng: run("sp", eng))
        self.n_ops = len(ops)
        self.n_sems = len(streams)


def _t5_bucket_np(rel):
    half, max_exact = 16, 8
    n = np.abs(rel)
    n_f = np.maximum(n, 1).astype(np.float32)
    val = (np.log(n_f / np.float32(max_exact)) / np.float32(math.log(1024 / max_exact))
           * np.float32(half - max_exact)).astype(np.float32)
    large = max_exact + val.astype(np.int32)
    large = np.minimum(large, half - 1)
    return np.where(rel > 0, half, 0) + np.where(n < max_exact, n, large)


A_OFF = dict(aq=0, ak=256, av=384, bb=512, bc=768, bh=1024, cq=1280, ckv=1536, ckr=1664, d0=1696)
D_PAT = ((128, 1), (512, 4), (2048, 16))
UA = 0
UD = 768
UCL = 768 + 2304
UB = UCL + 448
WUC = UB + 768


def _wu_cols():
    cols = []
    for h in range(4):
        cols += list(range(A_OFF["aq"] + h * 64, A_OFF["aq"] + h * 64 + 64))
        cols += list(range(A_OFF["av"] + (h // 2) * 64, A_OFF["av"] + (h // 2) * 64 + 64))
        cols += list(range(A_OFF["ak"] + (h // 2) * 64, A_OFF["ak"] + (h // 2) * 64 + 64))
    for h in range(4):
        for g in range(3):
            base = A_OFF["d0"] + g * 768
            for part in (0, 2, 1):
                cols += list(range(base + part * 256 + h * 64, base + part * 256 + h * 64 + 64))
    cols += list(range(A_OFF["cq"], A_OFF["cq"] + 256))
    cols += list(range(A_OFF["ckv"], A_OFF["ckv"] + 128))
    cols += list(range(A_OFF["ckr"], A_OFF["ckr"] + 32))
    cols += list(range(A_OFF["ckr"] + 16, A_OFF["ckr"] + 32)) + list(range(A_OFF["ckr"], A_OFF["ckr"] + 16))
    for ch in range(2):
        for nm in ("bb", "bc", "bh"):
            cols += list(range(A_OFF[nm] + ch * 128, A_OFF[nm] + ch * 128 + 128))
    assert len(cols) == WUC
    return np.array(cols, dtype=np.int64)


def _bias_tables(rel_bias):
    rb = np.asarray(rel_bias, dtype=np.float32)
    p = np.arange(128)[:, None]
    q = np.arange(128)[None, :]
    bA = np.empty((4, 128, 3, 128), np.float32)
    for h in range(4):
        for i in range(3):
            rel = 128 * i + p - 128 - q
            v = rb[_t5_bucket_np(rel), h]
            bA[h, :, i, :] = np.where(np.abs(rel) <= 128, v, np.float32(NEGM))
    bD = np.empty((12, 128, 6, 128), np.float32)
    for h in range(4):
        for g, (w, d) in enumerate(D_PAT):
            col = 4 + g * 4 + h
            rel1 = p - 64 - q
            rel2 = 64 + p - q
            t1 = np.where(np.abs(rel1) <= 64, rb[_t5_bucket_np(rel1 * d), col], np.float32(NEGM)).astype(np.float32)
            t2 = np.where(np.abs(rel2) <= 64, rb[_t5_bucket_np(rel2 * d), col], np.float32(NEGM)).astype(np.float32)
            t1e = np.where(p < 64, np.float32(NEGM), t1).astype(np.float32)
            t2e = np.where(p >= 64, np.float32(NEGM), t2).astype(np.float32)
            u = h * 3 + g
            for i, t in enumerate((t1, t2, t1e, t2, t1, t2e)):
                bD[u, :, i, :] = t
    return bA, bD


def build(nlayers=NL, debug=False, phases=None):
    PH = phases if phases is not None else {"P1", "LAT", "MLA", "A", "D", "B", "P3", "P4"}
    nc = bass.Bass("TRN2", target_bir_lowering=False)
    P = Prog()

    def din(name, shape, dt=F32):
        return nc.dram_tensor(name, list(shape), dt, kind="ExternalInput").ap()

    x_in = din("x", [S, DM])
    cT_in = din("cT", [128, 8])
    pos_in = din("pos", [1, S], I32)
    cst_in = din("cst", [128, 4])
    idn_in = din("idn", [128, 128])
    w_mod = din("w_mod", [NL, DM, 6 * DM])
    b_mod = din("b_mod", [NL, 1, 6 * DM])
    norm_g = din("norm_g", [NL, 1, 4 * DM])
    wu_in = din("wu", [NL, DM, WUC])
    sink_in = din("sink", [NL, 128, 4])
    bconv_in = din("bconv", [NL, 128, 2, 3])
    gq_in = din("gq", [NL, 128, 2])
    gkv_in = din("gkv", [NL, 128, 1])
    wq_in = din("wq", [NL, 256, 4 * 192])
    wkv_in = din("wkv", [NL, 128, 512])
    w_out = din("w_out", [NL, DM, DM])
    w_up = din("w_up", [NL, DM, 2 * DFF])
    fconv_in = din("fconv", [NL, 128, 44, 3])
    w_down = din("w_down", [NL, DFF, DM])
    biasA_in = din("biasA", [4, 128, 3, 128])
    biasD_in = din("biasD", [12, 128, 6, 128])
    out = nc.dram_tensor("out", [S, DM], F32, kind="ExternalOutput").ap()
    xmid = nc.dram_tensor("xmid", [S, DM], F32, kind="Internal").ap()
    mixd = nc.dram_tensor("mixd", [DM, S], BF16, kind="Internal").ap()
    dbg = None
    if debug:
        dbg = nc.dram_tensor("dbg", [DM, S], BF16, kind="ExternalOutput").ap()
    xname = {id(x_in): "xin", id(out): "out", id(xmid): "xmid"}

    cur = [(nc.sbuf_base + 63) // 64 * 64]
    top = nc.sbuf_top

    def alloc(name, shape, dt, at=None):
        per = int(np.prod(shape[1:])) * (2 if dt == BF16 else 4)
        per = (per + 63) // 64 * 64
        if at is None:
            off = cur[0]
            cur[0] += per
            assert cur[0] <= top, (name, cur[0], top)
        else:
            off = at
        return nc.alloc_sbuf_tensor_at(name, list(shape), dt, offset=off), off, per

    def T(name, shape, dt):
        return alloc(name, shape, dt)[0]

    identf = T("identf", [128, 128], F32)
    identb = T("identb", [128, 128], BF16)
    onesb = T("onesb", [128, 128], BF16)
    ones1 = T("ones1", [1, 128], F32)
    cst = T("cst", [128, 4], F32)
    epsb = T("epsb", [128, 1], F32)
    cact = T("cact", [128, 16], F32)
    s1T = T("s1T", [128, 8], F32)
    sh1T = T("sh1T", [128, 8], F32)
    s2T = T("s2T", [128, 8], F32)
    sh2T = T("sh2T", [128, 8], F32)
    G1B = T("G1B", [128, DM], F32)
    G2B = T("G2B", [128, DM], F32)
    esink = T("esink", [128, 4], F32)
    bconv = T("bconv", [128, 2, 3], F32)
    gq = T("gq", [128, 2], F32)
    gkv = T("gkv", [128, 1], F32)
    fconv = T("fconv", [128, 44, 3], F32)
    wkv = T("wkv", [128, 512], BF16)
    stat = T("stat", [128, 16], F32)
    xt = [T("xt%d" % i, [128, DM], F32) for i in range(2)]
    xn = [T("xn%d" % i, [128, DM], BF16) for i in range(2)]
    junk = T("junk", [128, DM], BF16)
    etmp = T("etmp", [128, DM], F32)
    wu = [T("wu%d" % i, [128, 8, 384], BF16) for i in range(2)]

    hT, offH, szH = alloc("hT", [128, 8, S], BF16)
    GT = alloc("GT", [128, 22, 1036], BF16, at=offH)[0]
    szGT = (22 * 1036 * 2 + 63) // 64 * 64
    HT2C = 1154
    hT2 = alloc("hT2", [128, 8, HT2C], BF16, at=offH + szGT)[0]
    assert szGT + 8 * HT2C * 2 <= szH
    wm = [alloc("wm%d" % i, [128, 8, 512], F32, at=offH + i * 16384)[0] for i in range(2)]
    wo = alloc("wo", [128, 8, DM], BF16, at=offH)[0]
    mixl = [alloc("mixl%d" % i, [128, 8, 512], BF16, at=offH + 16384 + i * 8192)[0] for i in range(2)]

    offU = cur[0]
    c = offU
    modB, _, sz = alloc("modB", [128, 6 * DM], F32, at=c); c += sz
    cactB, _, sz = alloc("cactB", [128, 8, 128], F32, at=c); c += sz
    rowv, _, sz = alloc("rowv", [1, 10 * DM], F32, at=c); c += sz
    tmpA, _, sz = alloc("tmpA", [128, DM], F32, at=c); c += sz
    endU_setup = c
    c = offU
    KCOLS = S + 2 * PADK
    QT, _, sz = alloc("QT", [128, S], BF16, at=c); c += sz
    KT, _, sz = alloc("KT", [128, KCOLS], BF16, at=c); c += sz
    VV, _, sz = alloc("VV", [128, 33, 128], BF16, at=c); c += sz
    OT, _, sz = alloc("OT", [128, S], BF16, at=c); c += sz
    acc, _, szacc = alloc("acc", [128, S], F32, at=c)
    cqnT, _, sz1 = alloc("cqnT", [128, 2, S], BF16, at=c)
    ckvnT, _, sz2 = alloc("ckvnT", [128, S], BF16, at=c + sz1)
    c += max(szacc, sz1 + sz2)
    biasT = []
    for i in range(2):
        t, _, sz = alloc("biasT%d" % i, [128, 6, 128], F32, at=c); c += sz
        biasT.append(t)
    sbt = []
    for i in range(2):
        t, _, sz = alloc("sbt%d" % i, [128, 512], F32, at=c); c += sz
        sbt.append(t)
    PT = []
    for i in range(3):
        t, _, sz = alloc("PT%d" % i, [128, 512], BF16, at=c); c += sz
        PT.append(t)
    dens, _, sz = alloc("dens", [128, 512], F32, at=c); c += sz
    rt1, _, sz = alloc("rt1", [128, 512], F32, at=c); c += sz
    rt2, _, sz = alloc("rt2", [128, 512], F32, at=c); c += sz
    sqb = []
    for i in range(3):
        t, _, sz = alloc("sqb%d" % i, [128, 512], BF16, at=c); c += sz
        sqb.append(t)
    wqh, _, sz = alloc("wqh", [128, 2, 192], BF16, at=c); c += sz
    TC, _, sz = alloc("TC", [128, S], BF16, at=c); c += sz
    TS, _, sz = alloc("TS", [128, S], BF16, at=c); c += sz
    endU_mix = c
    c = offU
    wdn, _, sz = alloc("wdn", [128, 22, DM], BF16, at=c); c += sz
    wup = []
    for i in range(2):
        t, _, sz = alloc("wup%d" % i, [128, 8, 1024], BF16, at=c); c += sz
        wup.append(t)
    tg, tv, tgg = [], [], []
    for i in range(2):
        t, _, sz = alloc("tg%d" % i, [128, 512], F32, at=c); c += sz
        tg.append(t)
        t, _, sz = alloc("tv%d" % i, [128, 512], F32, at=c); c += sz
        tv.append(t)
        t, _, sz = alloc("tgg%d" % i, [128, 512], BF16, at=c); c += sz
        tgg.append(t)
    fst = []
    for i in range(2):
        t, _, sz = alloc("fst%d" % i, [128, 2048], F32, at=c); c += sz
        fst.append(t)
    endU_ffn = c
    endU = max(endU_setup, endU_mix, endU_ffn)
    assert endU <= top, (endU, top, endU_setup - offU, endU_mix - offU, endU_ffn - offU)

    pb = [nc.alloc_psum_tensor("pb%d" % i, [128, 512], F32) for i in range(7)]
    pbT = nc.alloc_psum_tensor("pbT", [128, 8, 128], BF16)

    cnt = {"x": 0, "wu": 0, "uid": 0}

    def xtags(dr, t0, m):
        return [("X", xname[id(dr)], b) for b in range(t0 // 128, (t0 + m - 1) // 128 + 1)]

    P.dma("c_idn", lambda e: e.dma_start(out=identf[:], in_=idn_in), w=["identf"])
    P.dma("c_cst", lambda e: e.dma_start(out=cst[:], in_=cst_in), w=["cst"])
    P.dve(lambda e: e.tensor_copy(out=identb[:], in_=identf[:]), r=["identf"], w=["identb"])
    P.pool(lambda e: e.memset(onesb[:], 1.0), w=["onesb"])
    P.pool(lambda e: e.memset(ones1[:], 1.0), w=["ones1"])
    P.pool(lambda e: e.memset(epsb[:], EPS), w=["epsb"])
    P.dma("c_c", lambda e: e.dma_start(out=cact[:, 0:8], in_=cT_in), w=["cact"])
    P.act(lambda e: e.activation(out=cact[:, 8:16], in_=cact[:, 0:8], func=AF.Silu), r=["cact"], w=["cact"])
    P.barrier()

    def build_rope():
        posi = xt[1]
        posf = etmp
        TWO_PI = 2.0 * math.pi
        for piece in range(4 if "NOROPE" not in PH else 0):
            P.dma("c_pos", lambda e, piece=piece: e.dma_start(out=posi[0:1, :].bitcast(I32), in_=pos_in[:, piece * DM:(piece + 1) * DM]),
                  r=[("xt", 1)], w=[("xt", 1)])
            P.dve(lambda e: e.tensor_copy(out=posf[0:1, :], in_=posi[0:1, :].bitcast(I32)), r=[("xt", 1)], w=[("etmp", 0), ("etmp", 1)])
            for half in range(2):
                col0 = piece * DM + half * 512
                ang = pb[half][:]
                P.pe(lambda e, half=half, ang=ang: e.matmul(ang, lhsT=ones1[0:1, :], rhs=posf[0:1, half * 512:(half + 1) * 512],
                                                            start=True, stop=True), r=[("etmp", 0), ("etmp", 1), "ones1"], w=[("pb", half)])
                for which in (0, 1):
                    a = sbt[which]
                    ki = rt1 if which == 0 else rt2
                    P.dve(lambda e, a=a, ang=ang, which=which: e.tensor_scalar(
                        out=a[:], in0=ang, scalar1=cst[:, 0:1], scalar2=(0.5 * math.pi if which else 0.0),
                        op0=ALU.mult, op1=ALU.add), r=[("pb", half), "cst"], w=[("rr", which)])
                    P.dve(lambda e, a=a, ki=ki: e.tensor_scalar(out=ki[:].bitcast(I32), in0=a[:], scalar1=1.0 / TWO_PI,
                                                                scalar2=None, op0=ALU.mult), r=[("rr", which)], w=[("rk", which)])
                    P.dve(lambda e, ki=ki: e.tensor_copy(out=ki[:], in_=ki[:].bitcast(I32)), r=[("rk", which)], w=[("rk", which)])
                    P.dve(lambda e, a=a, ki=ki: e.scalar_tensor_tensor(out=a[:], in0=ki[:], scalar=-TWO_PI, in1=a[:],
                                                                       op0=ALU.mult, op1=ALU.add),
                          r=[("rk", which), ("rr", which)], w=[("rr", which)])
                    P.dve(lambda e, a=a: e.tensor_scalar(out=a[:], in0=a[:], scalar1=-3.14159, scalar2=3.14159,
                                                         op0=ALU.max, op1=ALU.min), r=[("rr", which)], w=[("rr", which)])
                    P.act(lambda e, a=a: e.activation(out=a[:], in_=a[:], func=AF.Sin), r=[("rr", which)], w=[("rr", which)])
                    if which == 0:
                        P.dve(lambda e, a=a, col0=col0: e.tensor_scalar(out=TS[:, col0:col0 + 512], in0=a[:], scalar1=cst[:, 1:2],
                                                                        scalar2=None, op0=ALU.mult), r=[("rr", 0), "cst"], w=["TS"])
                    else:
                        P.dve(lambda e, a=a, col0=col0: e.tensor_copy(out=TC[:, col0:col0 + 512], in_=a[:]), r=[("rr", 1)], w=["TC"])


    def load_x_block(src, t0, m):
        i = cnt["x"] % 2
        cnt["x"] += 1
        P.dma("ldx%d" % i, lambda e: e.dma_start(out=xt[i][0:m, :], in_=src[t0:t0 + m, :]), r=xtags(src, t0, m), w=[("xt", i)])
        return i

    def rstd_inplace(ap, m, n_inv, rtag):
        P.act(lambda e: e.activation(out=ap, in_=ap, func=AF.Sqrt, scale=n_inv, bias=epsb[0:m, 0:1]), r=[rtag, "epsb"], w=[rtag])
        P.dve(lambda e: e.reciprocal(out=ap, in_=ap), r=[rtag], w=[rtag])

    def norm_stage1(src, t0, m):
        i = load_x_block(src, t0, m)
        sc_ = i
        stg_ = "statn%d" % sc_
        P.pool(lambda e: e.memset(stat[:, sc_:sc_ + 1], 0.0), w=[stg_])
        P.act(lambda e: e.activation(out=junk[0:m, :], in_=xt[i][0:m, :], func=AF.Square, accum_out=stat[0:m, sc_:sc_ + 1]),
              r=[("xt", i), stg_], w=[stg_, "junk"])
        rstd_inplace(stat[0:m, sc_:sc_ + 1], m, 1.0 / DM, stg_)
        P.dve(lambda e: e.tensor_scalar(out=xn[i][0:m, :], in0=xt[i][0:m, :], scalar1=stat[0:m, sc_:sc_ + 1], scalar2=None,
                                        op0=ALU.mult), r=[("xt", i), stg_], w=[("xn", i)])
        return i

    def norm_stage2(i, m, sT, shT, dest, dcol, tagfn):
        pT = pbT[:]
        for k in range(8):
            P.pe(lambda e, k=k: e.transpose(out=pT[:, k, 0:m], in_=xn[i][0:m, k * 128:(k + 1) * 128], identity=identb[0:m, 0:m]),
                 r=[("xn", i), "identb"], w=[("pb", 7)])
        for k in range(8):
            P.act(lambda e, k=k: e.activation(out=dest[:, k, dcol:dcol + m], in_=pT[:, k, 0:m], func=AF.Identity,
                                              scale=sT[:, k:k + 1], bias=shT[:, k:k + 1]),
                  r=[("pb", 7), "modT"], w=[tagfn(0)])

    def norm_blocks(src, blocks, sT, shT, dest, dcol_fn, tag_fn):
        pend = None
        for j, bblk in enumerate(blocks):
            i = norm_stage1(src, bblk * 128, 128)
            if pend is not None:
                norm_stage2(*pend)
            pend = (i, 128, sT, shT, dest, dcol_fn(j, bblk), (lambda e_, j=j, bblk=bblk: tag_fn(j, bblk, e_)))
        if pend is not None:
            norm_stage2(*pend)

    def epilogue(psy0, psy1, ptag, src, dst, t0, m, GB):
        i = load_x_block(src, t0, m)
        so = 2 + 2 * (cnt["x"] % 2)
        st2 = "stat%d" % so
        P.act(lambda e: e.memzero(stat[:, so:so + 2]), w=[st2])
        P.act(lambda e: e.activation(out=junk[0:m, 0:512], in_=psy0[0:m, :], func=AF.Square, accum_out=stat[0:m, so:so + 1]),
              r=[ptag[0], st2], w=[st2, "junk"])
        P.act(lambda e: e.activation(out=junk[0:m, 512:1024], in_=psy1[0:m, :], func=AF.Square, accum_out=stat[0:m, so + 1:so + 2]),
              r=[ptag[1], st2], w=[st2, "junk"])
        P.dve(lambda e: e.tensor_tensor(out=stat[0:m, so:so + 1], in0=stat[0:m, so:so + 1], in1=stat[0:m, so + 1:so + 2], op=ALU.add),
              r=[st2], w=[st2])
        rstd_inplace(stat[0:m, so:so + 1], m, 1.0 / DM, st2)
        for hf, psy in enumerate((psy0, psy1)):
            sl = slice(hf * 512, (hf + 1) * 512)
            P.dve(lambda e, psy=psy, sl=sl: e.scalar_tensor_tensor(out=etmp[0:m, sl], in0=psy[0:m, :], scalar=stat[0:m, so:so + 1],
                                                                    in1=GB[0:m, sl], op0=ALU.mult, op1=ALU.mult),
                  r=[ptag[hf], st2, "GB"], w=[("etmp", hf)])
            P.pool(lambda e, sl=sl: e.tensor_tensor(out=xt[i][0:m, sl], in0=xt[i][0:m, sl], in1=etmp[0:m, sl], op=ALU.add),
                   r=[("etmp", hf), ("xt", i)], w=[("xt", i)])
        P.dma("stx%d" % i, lambda e: e.dma_start(out=dst[t0:t0 + m, :], in_=xt[i][0:m, :]), r=[("xt", i)], w=xtags(dst, t0, m), eng="pool")

    def stage_cast(chan_base, src_ap, nelem_shape, dst_ap, dtags):
        i = cnt["x"] % 2
        cnt["x"] += 1
        a_, b_ = nelem_shape
        sv = xt[i][:, 0:a_ * b_].rearrange("p (a b) -> p a b", a=a_) if a_ > 1 else xt[i][:, 0:b_]
        P.dma("ldx%d" % i, lambda e: e.dma_start(out=sv, in_=src_ap), w=[("xt", i)])
        P.pool(lambda e: e.tensor_copy(out=dst_ap, in_=sv), r=[("xt", i)], w=dtags)

    def load_wu(l, c0, ncols):
        wi = cnt["wu"] % 2
        cnt["wu"] += 1
        kstep = {192: 4, 384: 2, 64: 8}[ncols]
        for kk in range(0, 8, kstep):
            stage_cast("wu", wu_in[l][kk * 128:(kk + kstep) * 128, c0:c0 + ncols].rearrange("(k p) n -> p k n", p=128),
                       (kstep, ncols), wu[wi][:, kk:kk + kstep, 0:ncols], [("wu", wi)])
        return wi

    pj = [0]

    def proj_fm(wi, c0, M, tile_j, evac):
        b = pj[0] % 2
        pj[0] += 1
        ps = pb[b][:]
        for k in range(8):
            P.pe(lambda e, k=k: e.matmul(ps[0:M, :], lhsT=wu[wi][:, k, c0:c0 + M], rhs=hT[:, k, tile_j * 512:(tile_j + 1) * 512],
                                         start=(k == 0), stop=(k == 7)), r=[("wu", wi)], w=[("pb", b)])
        evac(ps, ("pb", b))

    def perm_dst(rows_ap, col_off, d, j):
        if d == 1:
            return rows_ap[:, col_off + j * 512: col_off + (j + 1) * 512]
        n = 512 // d
        return rows_ap[:, col_off:col_off + S].rearrange("p (r i) -> p r i", r=d)[:, :, j * n:(j + 1) * n]

    def perm_src(ps_rows, d):
        if d == 1:
            return ps_rows
        return ps_rows.rearrange("p (i r) -> p r i", r=d)

    def v_segments(m, d, koff):
        Ld = S // d
        q0 = koff + 128 * m
        segs = []
        for half in range(2):
            qs = q0 + 64 * half
            if qs < 0 or qs >= S:
                continue
            r, i = divmod(qs, Ld)
            tok = i * d + r
            if segs and segs[-1][1] == 64 and segs[-1][0] == 0 and segs[-1][2] + 64 * d == tok and (qs % Ld) != 0:
                segs[-1] = (0, 128, segs[-1][2])
            else:
                segs.append((64 * half, 64, tok))
        return segs

    vslot = [0]

    def proj_v(wi, c0, d, koff, nblk):
        segs = []
        for m in range(nblk):
            for (row0, nr, tok) in v_segments(m, d, koff):
                segs.append((m, row0, nr, tok))
        for g0_ in range(0, len(segs), 8):
            grp = segs[g0_:g0_ + 8]
            gi = vslot[0]
            vslot[0] += 1
            bank = (2, 4)[gi % 2]
            ps = pb[bank][:]
            for si, (m, row0, nr, tok) in enumerate(grp):
                cs = si * 64
                tsl = slice(tok, tok + (nr - 1) * d + 1, d)
                for k in range(8):
                    P.pe(lambda e, k=k, cs=cs, nr=nr, tsl=tsl, ps=ps: e.matmul(
                        ps[0:nr, cs:cs + 64], lhsT=hT[:, k, tsl], rhs=wu[wi][:, k, c0:c0 + 64], start=(k == 0), stop=(k == 7)),
                        r=[("wu", wi)], w=[("pb", bank)])
            for si, (m, row0, nr, tok) in enumerate(grp):
                cs = si * 64
                if gi % 2 == 0:
                    P.act(lambda e, cs=cs, nr=nr, row0=row0, m=m, ps=ps: e.copy(out=VV[row0:row0 + nr, m, 0:64], in_=ps[0:nr, cs:cs + 64]),
                          r=[("pb", bank), "VVi"], w=[("VV", m, row0)] + ([("VV", m, 64)] if nr == 128 else []))
                else:
                    P.dve(lambda e, cs=cs, nr=nr, row0=row0, m=m, ps=ps: e.tensor_copy(out=VV[row0:row0 + nr, m, 0:64], in_=ps[0:nr, cs:cs + 64]),
                          r=[("pb", bank), "VVi"], w=[("VV", m, row0)] + ([("VV", m, 64)] if nr == 128 else []))

    def vtags(m):
        return [("VV", m, 0), ("VV", m, 64)]

    def finalize_block(ops, otag, ncols, dst_ap, sink_ap):
        if sink_ap is not None:
            P.dve(lambda e: e.tensor_scalar(out=dens[0:64, 0:ncols], in0=ops[64:128, 0:ncols], scalar1=sink_ap, scalar2=None,
                                            op0=ALU.add), r=[otag, "esink"], w=["dens"])
        else:
            P.dve(lambda e: e.tensor_copy(out=dens[0:64, 0:ncols], in_=ops[64:128, 0:ncols]), r=[otag], w=["dens"])
        P.dve(lambda e: e.reciprocal(out=dens[0:64, 0:ncols], in_=dens[0:64, 0:ncols]), r=["dens"], w=["dens"])
        P.dve(lambda e: e.tensor_tensor(out=dst_ap, in0=ops[0:64, 0:ncols], in1=dens[0:64, 0:ncols], op=ALU.mult),
              r=[otag, "dens"], w=["OT"])

    def store_mix(row0, nrows):
        P.dma("stmix", lambda e: e.dma_start(out=mixd[row0:row0 + nrows, :], in_=OT[0:nrows, :]), r=["OT"], w=[("mixd", row0)])

    def banded_unit(l, kind, h, g, ucol, need_kv, first, last):
        if kind == "A":
            d, koff, nblk, ntile = 1, 0, 32, 3
            bsrc = biasA_in[h]
        else:
            d = D_PAT[g][1]
            koff, nblk, ntile = -PADK, 33, 6
            bsrc = biasD_in[h * 3 + g]
        bi = cnt["uid"] % 2
        cnt["uid"] += 1
        P.dma("ldb%d" % bi, lambda e: e.dma_start(out=biasT[bi][:, 0:ntile, :], in_=bsrc), w=[("biasT", bi)])
        wi = load_wu(l, ucol, 192)
        for j in range(8):
            def evq(ps, tag, j=j):
                P.act(lambda e: e.activation(out=perm_dst(QT[64:128, :], 0, d, j), in_=perm_src(ps[0:64, :], d), func=AF.Copy, scale=0.125),
                      r=[tag], w=["QT"])
            proj_fm(wi, 0, 64, j, evq)
            if need_kv:
                def evkv(ps, tag, j=j):
                    P.dve(lambda e: e.tensor_copy(out=perm_dst(KT[:, :], PADK, d, j), in_=perm_src(ps[:, :], d)),
                          r=[tag], w=["KT"])
                proj_fm(wi, 64, 128, j, evkv)
        if need_kv:
            pTv = pbT[:]
            for g8 in range(0, nblk, 8):
                nb_ = min(8, nblk - g8)
                for si in range(nb_):
                    kc = PADK + koff + 128 * (g8 + si)
                    P.pe(lambda e, si=si, kc=kc: e.transpose(out=pTv[:, si, 0:64], in_=KT[0:64, kc:kc + 128], identity=identb[0:64, 0:64]),
                         r=["KT", "identb"], w=[("pb", 7)])
                vt = []
                for si in range(nb_):
                    vt += vtags(g8 + si)
                P.act(lambda e, g8=g8, nb_=nb_: e.copy(out=VV[:, g8:g8 + nb_, 0:64], in_=pTv[:, 0:nb_, 0:64]), r=[("pb", 7), "VVi"], w=vt)
        bpr = 32 // d

        def kbs_of(b):
            if kind == "A":
                return [(b - 1 + i, i) for i in range(3) if 0 <= b - 1 + i < 32]
            if b % bpr == 0:
                t0_ = 2
            elif b % bpr == bpr - 1:
                t0_ = 4
            else:
                t0_ = 0
            return [(b, t0_), (b + 1, t0_ + 1)]

        def stage_a(b):
            kbs = kbs_of(b)
            nk = len(kbs)
            sb_ = (4, 5, 0, 3)[b % 4]
            sps = pb[sb_][:, 0:384].rearrange("p (a b) -> p a b", a=3)
            for i, (m, ti) in enumerate(kbs):
                kc = PADK + koff + 128 * m
                P.pe(lambda e, i=i, kc=kc, sps=sps, b=b: e.matmul(sps[:, i, :], lhsT=KT[64:128, kc:kc + 128], rhs=QT[64:128, b * 128:(b + 1) * 128],
                                                                  start=True, stop=True), r=["QT", "KT"], w=[("pb", sb_)])
            tfirst = kbs[0][1]
            sbuf = (sbt[0], sbt[1], rt1, rt2)[b % 4]
            stag = (("sbt", 0), ("sbt", 1), "rt1", "rt2")[b % 4]
            sv = sbuf[:, 0:nk * 128].rearrange("p (a b) -> p a b", a=nk)
            P.dve(lambda e, sv=sv, tfirst=tfirst, nk=nk, sps=sps: e.tensor_tensor(out=sv, in0=sps[:, 0:nk, :], in1=biasT[bi][:, tfirst:tfirst + nk, :],
                                                                                  op=ALU.add),
                  r=[("pb", sb_), ("biasT", bi)], w=[stag])
            pt = (PT[0], PT[1], PT[2], sqb[2])[b % 4]
            ptag = (("PT", 0), ("PT", 1), ("PT", 2), ("sqb", 2))[b % 4]
            P.act(lambda e, sbuf=sbuf, pt=pt, nk=nk: e.activation(out=pt[:, 0:nk * 128], in_=sbuf[:, 0:nk * 128], func=AF.Exp),
                  r=[stag], w=[ptag])

        def stage_b(b):
            kbs = kbs_of(b)
            nk = len(kbs)
            pt = (PT[0], PT[1], PT[2], sqb[2])[b % 4]
            ptag = (("PT", 0), ("PT", 1), ("PT", 2), ("sqb", 2))[b % 4]
            ob = (6, 1)[b % 2]
            ops = pb[ob][:]
            for i, (m, ti) in enumerate(kbs):
                P.pe(lambda e, i=i, m=m, pt=pt, ops=ops, nk=nk: e.matmul(ops[:, 0:128], lhsT=VV[:, m, :], rhs=pt[:, i * 128:(i + 1) * 128],
                                                                       start=(i == 0), stop=(i == nk - 1)),
                     r=vtags(m) + [ptag], w=[("pb", ob)])
            if kind == "A":
                finalize_block(ops, ("pb", ob), 128, OT[0:64, b * 128:(b + 1) * 128], esink[64:128, h:h + 1])
            else:
                Ld = S // d
                r_, i0 = divmod(128 * b, Ld)
                c0 = r_ + d * i0
                asl = acc[:, c0:c0 + 127 * d + 1:d]
                if first:
                    P.dve(lambda e, asl=asl, ops=ops: e.tensor_copy(out=asl, in_=ops[:, 0:128]), r=[("pb", ob)], w=["acc"])
                else:
                    P.dve(lambda e, asl=asl, ops=ops: e.tensor_tensor(out=asl, in0=ops[:, 0:128], in1=asl, op=ALU.add),
                          r=[("pb", ob), "acc"], w=["acc"])

        LAB = 3
        for b in range(LAB):
            stage_a(b)
        for b in range(32):
            if b + LAB < 32:
                stage_a(b + LAB)
            stage_b(b)
        if kind == "A":
            store_mix(h * 64, 64)
        elif last:
            for q4 in range(4):
                sl = slice(q4 * 1024, (q4 + 1) * 1024)
                P.act(lambda e, sl=sl: e.copy(out=etmp[0:64, :], in_=acc[64:128, sl]), r=["acc"], w=[("etmp", 0), ("etmp", 1)])
                P.dve(lambda e: e.reciprocal(out=etmp[0:64, :], in_=etmp[0:64, :]), r=[("etmp", 0)], w=[("etmp", 0), ("etmp", 1)])
                P.dve(lambda e, sl=sl: e.tensor_tensor(out=OT[0:64, sl], in0=acc[0:64, sl], in1=etmp[0:64, :], op=ALU.mult),
                      r=["acc", ("etmp", 0)], w=["OT"])
            store_mix(768 + h * 64, 64)

    def latent_unit(l):
        wi = load_wu(l, UCL, 384)
        for j in range(8):
            tsl = slice(j * 512, (j + 1) * 512)
            pss = []
            for ci, (c0, bank) in enumerate(((0, 0), (128, 1), (256, 3))):
                ps = pb[bank][:]
                for k in range(8):
                    P.pe(lambda e, k=k, ps=ps, c0=c0, tsl=tsl, wi=wi: e.matmul(ps, lhsT=wu[wi][:, k, c0:c0 + 128], rhs=hT[:, k, tsl],
                                                               start=(k == 0), stop=(k == 7)), r=[("wu", wi)], w=[("pb", bank)])
                P.act(lambda e, ps=ps, ci=ci: e.activation(out=sqb[ci][:], in_=ps, func=AF.Square), r=[("pb", bank)], w=[("sqb", ci)])
                pss.append((ps, ("pb", bank)))
            sq, sk = pb[4][:], pb[5][:]
            P.pe(lambda e: e.matmul(sq, lhsT=onesb[:], rhs=sqb[0][:], start=True, stop=False), r=[("sqb", 0), "onesb"], w=[("pb", 4)])
            P.pe(lambda e: e.matmul(sq, lhsT=onesb[:], rhs=sqb[1][:], start=False, stop=True), r=[("sqb", 1), "onesb"], w=[("pb", 4)])
            P.pe(lambda e: e.matmul(sk, lhsT=onesb[:], rhs=sqb[2][:], start=True, stop=True), r=[("sqb", 2), "onesb"], w=[("pb", 5)])
            for (srcp, stag, n_inv, dstt, tag) in ((sq, ("pb", 4), 1.0 / 256, rt1, "rt1"), (sk, ("pb", 5), 1.0 / 128, rt2, "rt2")):
                P.act(lambda e, srcp=srcp, n_inv=n_inv, dstt=dstt: e.activation(out=dstt[:], in_=srcp, func=AF.Sqrt, scale=n_inv, bias=epsb[:, 0:1]),
                      r=[stag, "epsb"], w=[tag])
                P.dve(lambda e, dstt=dstt: e.reciprocal(out=dstt[:], in_=dstt[:]), r=[tag], w=[tag])
            for ci in range(2):
                P.dve(lambda e, ci=ci, tsl=tsl, pss=pss: e.scalar_tensor_tensor(out=cqnT[:, ci, tsl], in0=pss[ci][0], scalar=gq[:, ci:ci + 1], in1=rt1[:],
                                                              op0=ALU.mult, op1=ALU.mult), r=[pss[ci][1], "rt1", "gq"], w=["cqnT"])
            P.dve(lambda e, tsl=tsl, pss=pss: e.scalar_tensor_tensor(out=ckvnT[:, tsl], in0=pss[2][0], scalar=gkv[:, 0:1], in1=rt2[:],
                                                   op0=ALU.mult, op1=ALU.mult), r=[pss[2][1], "rt2", "gq"], w=["ckvnT"])
        wi = load_wu(l, UCL + 384, 64)
        for j in range(8):
            tsl = slice(j * 512, (j + 1) * 512)
            for ab, bank in ((0, 6), (1, 0)):
                ps = pb[bank][:]
                for k in range(8):
                    P.pe(lambda e, k=k, ps=ps, ab=ab, tsl=tsl, wi=wi: e.matmul(ps[0:32, :], lhsT=wu[wi][:, k, ab * 32:(ab + 1) * 32], rhs=hT[:, k, tsl],
                                                               start=(k == 0), stop=(k == 7)), r=[("wu", wi)], w=[("pb", bank)])
            P.dve(lambda e, tsl=tsl: e.tensor_tensor(out=rt1[0:32, :], in0=pb[6][0:32, :], in1=TC[0:32, tsl], op=ALU.mult),
                  r=[("pb", 6), "TC"], w=["rt1"])
            P.dve(lambda e, tsl=tsl: e.tensor_tensor(out=rt2[0:32, :], in0=pb[0][0:32, :], in1=TS[0:32, tsl], op=ALU.mult),
                  r=[("pb", 0), "TS"], w=["rt2"])
            P.dve(lambda e, j=j: e.tensor_tensor(out=KT[64:96, PADK + j * 512:PADK + (j + 1) * 512], in0=rt1[0:32, :], in1=rt2[0:32, :],
                                                 op=ALU.add), r=["rt1", "rt2"], w=["KT"])

    def mla_unit(l, h):
        scale = 96.0 ** -0.5
        stage_cast("wq", wq_in[l][:, h * 192:(h + 1) * 192].rearrange("(k p) n -> p k n", p=128), (2, 192), wqh[:], ["wqh"])
        for j in range(8):
            tsl = slice(j * 512, (j + 1) * 512)
            pA, pB, pK = pb[0][:], pb[1][:], pb[3][:]
            for ab, ps, bank in ((0, pA, 0), (1, pB, 1)):
                for ci in range(2):
                    P.pe(lambda e, ps=ps, ab=ab, ci=ci, tsl=tsl: e.matmul(ps[0:96, :], lhsT=wqh[:, ci, ab * 96:(ab + 1) * 96], rhs=cqnT[:, ci, tsl],
                                                                          start=(ci == 0), stop=(ci == 1)), r=["wqh", "cqnT"], w=[("pb", bank)])
            P.pe(lambda e, tsl=tsl: e.matmul(pK[0:64, :], lhsT=wkv[:, h * 128:h * 128 + 64], rhs=ckvnT[:, tsl], start=True, stop=True),
                 r=["wkv", "ckvnT"], w=[("pb", 3)])
            P.act(lambda e, tsl=tsl: e.activation(out=QT[0:64, tsl], in_=pA[0:64, :], func=AF.Copy, scale=scale), r=[("pb", 0)], w=["QT"])
            P.dve(lambda e, tsl=tsl: e.tensor_tensor(out=rt1[64:96, :], in0=pA[64:96, :], in1=TC[64:96, tsl], op=ALU.mult),
                  r=[("pb", 0), "TC"], w=["rt1"])
            P.dve(lambda e, tsl=tsl: e.tensor_tensor(out=rt2[64:96, :], in0=pB[64:96, :], in1=TS[64:96, tsl], op=ALU.mult),
                  r=[("pb", 1), "TS"], w=["rt2"])
            P.dve(lambda e: e.tensor_tensor(out=rt1[64:96, :], in0=rt1[64:96, :], in1=rt2[64:96, :], op=ALU.add),
                  r=["rt1", "rt2"], w=["rt1"])
            P.act(lambda e, tsl=tsl: e.activation(out=QT[64:96, tsl], in_=rt1[64:96, :], func=AF.Copy, scale=scale), r=["rt1"], w=["QT"])
            P.act(lambda e, j=j: e.copy(out=KT[0:64, PADK + j * 512:PADK + (j + 1) * 512], in_=pK[0:64, :]), r=[("pb", 3)], w=["KT"])
        import os as _os
        _mv = int(_os.environ.get("MV", "9"))
        if _mv < 2:
            return
        for g4 in range(4):
            gi = vslot[0]
            vslot[0] += 1
            bank = (2, 4)[gi % 2]
            ps = pb[bank][:]
            for si in range(8):
                m = g4 * 8 + si
                P.pe(lambda e, m=m, si=si, ps=ps: e.matmul(ps[:, si * 64:(si + 1) * 64], lhsT=ckvnT[:, m * 128:(m + 1) * 128],
                                                         rhs=wkv[:, h * 128 + 64:h * 128 + 128], start=True, stop=True),
                     r=["wkv", "ckvnT"], w=[("pb", bank)])
            vt = []
            for si in range(8):
                vt += vtags(g4 * 8 + si)
            psv_ = ps.rearrange("p (a b) -> p a b", a=8)
            if gi % 2 == 0:
                P.act(lambda e, g4=g4, psv_=psv_: e.copy(out=VV[:, g4 * 8:(g4 + 1) * 8, 0:64], in_=psv_), r=[("pb", bank), "VVi"], w=vt)
            else:
                P.dve(lambda e, g4=g4, psv_=psv_: e.tensor_copy(out=VV[:, g4 * 8:(g4 + 1) * 8, 0:64], in_=psv_), r=[("pb", bank), "VVi"], w=vt)
        if _mv < 3:
            return
        items = [(qt, m) for qt in range(8) for m in range(32)]

        def mla_a(it):
            qt, m = items[it]
            qsl = slice(qt * 512, (qt + 1) * 512)
            sbk = (3, 4, 5, 0)[it % 4]
            pti = it % 4
            sps = pb[sbk][:]
            pt = (PT[0], PT[1], PT[2], sqb[2])[pti]
            kc = PADK + 128 * m
            P.pe(lambda e, sps=sps, kc=kc, qsl=qsl: e.matmul(sps, lhsT=KT[0:96, kc:kc + 128], rhs=QT[0:96, qsl], start=True, stop=True),
                 r=["QT", "KT"], w=[("pb", sbk)])
            P.act(lambda e, sps=sps, pt=pt: e.activation(out=pt[:], in_=sps, func=AF.Exp), r=[("pb", sbk)], w=[(("PT", 0), ("PT", 1), ("PT", 2), ("sqb", 2))[pti]])

        def mla_b(it):
            qt, m = items[it]
            qsl = slice(qt * 512, (qt + 1) * 512)
            pti = it % 4
            pt = (PT[0], PT[1], PT[2], sqb[2])[pti]
            ob = (6, 1)[qt % 2]
            ops = pb[ob][:]
            P.pe(lambda e, m=m, pt=pt, ops=ops: e.matmul(ops, lhsT=VV[:, m, :], rhs=pt[:], start=(m == 0), stop=(m == 31)),
                 r=vtags(m) + [(("PT", 0), ("PT", 1), ("PT", 2), ("sqb", 2))[pti]], w=[("pb", ob)])
            if m == 31 and _mv >= 4:
                finalize_block(ops, ("pb", ob), 512, OT[0:64, qsl], None)

        LA = 3
        for it in range(min(LA, len(items))):
            mla_a(it)
        for it in range(len(items)):
            if it + LA < len(items):
                mla_a(it + LA)
            mla_b(it)
        if _mv >= 5:
            store_mix(512 + h * 64, 64)

    def b_unit(l, ch):
        wi = load_wu(l, UB + ch * 384, 384)
        ZC = PADK
        for j in range(8):
            tsl = slice(j * 512, (j + 1) * 512)
            for which, bank in ((1, 0), (2, 1), (0, 3)):
                ps = pb[bank][:]
                for k in range(8):
                    P.pe(lambda e, k=k, ps=ps, which=which, tsl=tsl: e.matmul(ps, lhsT=wu[wi][:, k, which * 128:(which + 1) * 128], rhs=hT[:, k, tsl],
                                                                              start=(k == 0), stop=(k == 7)), r=[("wu", wi)], w=[("pb", bank)])
            P.act(lambda e: e.copy(out=sqb[0][:], in_=pb[0][:]), r=[("pb", 0)], w=[("sqb", 0)])
            P.dve(lambda e, j=j: e.tensor_tensor(out=KT[:, ZC + j * 512:ZC + (j + 1) * 512], in0=pb[1][:], in1=sqb[0][:], op=ALU.mult),
                  r=[("pb", 1), ("sqb", 0)], w=["KT"])
            P.act(lambda e, tsl=tsl: e.copy(out=QT[:, tsl], in_=pb[3][:]), r=[("pb", 3)], w=["QT"])
        for q4 in range(4):
            sl = slice(q4 * 1024, (q4 + 1) * 1024)
            zc = ZC + q4 * 1024
            P.act(lambda e, sl=sl, zc=zc: e.activation(out=acc[:, sl], in_=KT[:, zc:zc + 1024], func=AF.Identity, scale=bconv[:, ch, 1:2]),
                  r=["KT", "bconv", "acc"], w=["acc"])
            P.dve(lambda e, sl=sl, zc=zc: e.scalar_tensor_tensor(out=acc[:, sl], in0=KT[:, zc - 1:zc + 1023], scalar=bconv[:, ch, 0:1],
                                                                  in1=acc[:, sl], op0=ALU.mult, op1=ALU.add), r=["KT", "bconv", "acc"], w=["acc"])
            P.dve(lambda e, sl=sl, zc=zc: e.scalar_tensor_tensor(out=acc[:, sl], in0=KT[:, zc + 1:zc + 1025], scalar=bconv[:, ch, 2:3],
                                                                  in1=acc[:, sl], op0=ALU.mult, op1=ALU.add), r=["KT", "bconv", "acc"], w=["acc"])
            P.pool(lambda e, sl=sl: e.tensor_tensor(out=OT[:, sl], in0=acc[:, sl], in1=QT[:, sl], op=ALU.mult), r=["acc", "QT"], w=["OT"])
        store_mix(256 + ch * 128, 128)

    for l in range(nlayers):
        src = x_in if l == 0 else out
        P.phase = "S0"
        P.dma("ld_row", lambda e, l=l: e.dma_start(out=rowv[0:1, 0:6 * DM], in_=b_mod[l]), w=["rowv"])
        P.dma("ld_row", lambda e, l=l: e.dma_start(out=rowv[0:1, 6 * DM:10 * DM], in_=norm_g[l]), r=["rowv"], w=["rowv"])
        P.dma("ld_sm0", lambda e, l=l: e.dma_start(out=esink[:], in_=sink_in[l]), w=["esink"])
        P.dma("ld_sm1", lambda e, l=l: e.dma_start(out=bconv[:], in_=bconv_in[l]), w=["bconv"])
        P.dma("ld_sm2", lambda e, l=l: e.dma_start(out=gq[:], in_=gq_in[l]), w=["gq"])
        P.dma("ld_sm3", lambda e, l=l: e.dma_start(out=gkv[:], in_=gkv_in[l]), r=["gq"], w=["gq"])
        P.dma("ld_sm4", lambda e, l=l: e.dma_start(out=fconv[:], in_=fconv_in[l]), w=["fconv"])
        stage_cast("wkv", wkv_in[l], (1, 512), wkv[:], ["wkv"])
        P.act(lambda e: e.activation(out=esink[:], in_=esink[:], func=AF.Exp), r=["esink"], w=["esink"])
        P.pool(lambda e: e.memset(cactB[:], 1.0), w=["cactB"])
        for k in range(8):
            P.dve(lambda e, k=k: e.tensor_scalar(out=cactB[:, k, :], in0=cactB[:, k, :], scalar1=cact[:, 8 + k:9 + k],
                                                 scalar2=None, op0=ALU.mult), r=["cact", "cactB"], w=["cactB"])
        for n in range(12 if "NOMOD" not in PH else 0):
            bi = n % 2
            P.dma("ldwm%d" % bi, lambda e, l=l, n=n, bi=bi: e.dma_start(
                out=wm[bi][:], in_=w_mod[l][:, n * 512:(n + 1) * 512].rearrange("(k p) n -> p k n", p=128)), w=[("wm", bi)])
            ps = pb[bi][:]
            for k in range(8):
                P.pe(lambda e, k=k, ps=ps, bi=bi: e.matmul(ps, lhsT=cactB[:, k, :], rhs=wm[bi][:, k, :], start=(k == 0), stop=False),
                     r=[("wm", bi), "cactB"], w=[("pb", bi)])
            P.pe(lambda e, ps=ps, n=n: e.matmul(ps, lhsT=ones1[0:1, :], rhs=rowv[0:1, n * 512:(n + 1) * 512], start=False, stop=True),
                 r=["rowv", "ones1"], w=[("pb", bi)])
            if n % 2 == 0:
                P.act(lambda e, ps=ps, n=n: e.copy(out=modB[:, n * 512:(n + 1) * 512], in_=ps), r=[("pb", bi)], w=[("modB", n)])
            else:
                P.dve(lambda e, ps=ps, n=n: e.tensor_copy(out=modB[:, n * 512:(n + 1) * 512], in_=ps), r=[("pb", bi)], w=[("modB", n)])

        GBK = (3, 5)

        def gain_bcast(j):
            for hf in range(2):
                P.pe(lambda e, hf=hf: e.matmul(pb[GBK[hf]][:], lhsT=ones1[0:1, :],
                                               rhs=rowv[0:1, 6 * DM + j * DM + hf * 512:6 * DM + j * DM + (hf + 1) * 512],
                                               start=True, stop=True), r=["rowv", "ones1"], w=[("pb", GBK[hf])])

        def make_T(src_ap_fn, dstT, rtags):
            pT = pb[4][:].rearrange("p (a b) -> p a b", a=4)
            for half in range(2):
                for k4 in range(4):
                    k = half * 4 + k4
                    P.pe(lambda e, k=k, k4=k4: e.transpose(out=pT[:, k4, :], in_=src_ap_fn(k), identity=identf[:]),
                         r=list(rtags) + ["identf"], w=[("pb", 4)])
                P.dve(lambda e, half=half: e.tensor_copy(out=dstT[:, half * 4:(half + 1) * 4], in_=pT[:, :, 0]), r=[("pb", 4)], w=["modT"])

        def mtag(off):
            return [("modB", off // 512), ("modB", off // 512 + 1)]

        for (jn, sc_off, sh_off, g_off, jg, sT_, shT_, GB_) in () if "NODER" in PH else ((0, 1 * DM, 0, 2 * DM, 1, s1T, sh1T, G1B), (2, 4 * DM, 3 * DM, 5 * DM, 3, s2T, sh2T, G2B)):
            gain_bcast(jn)
            for hf in range(2):
                sl = slice(hf * 512, (hf + 1) * 512)
                P.dve(lambda e, hf=hf, sl=sl, sc_off=sc_off: e.scalar_tensor_tensor(
                    out=tmpA[:, sl], in0=modB[:, sc_off + hf * 512:sc_off + (hf + 1) * 512], scalar=1.0, in1=pb[GBK[hf]][:],
                    op0=ALU.add, op1=ALU.mult), r=mtag(sc_off) + [("pb", GBK[hf])], w=["tmpA"])
            make_T(lambda k: tmpA[:, k * 128:(k + 1) * 128], sT_, ["tmpA"])
            make_T(lambda k, sh_off=sh_off: modB[:, sh_off + k * 128:sh_off + (k + 1) * 128], shT_, mtag(sh_off))
            gain_bcast(jg)
            for hf in range(2):
                sl = slice(hf * 512, (hf + 1) * 512)
                P.dve(lambda e, hf=hf, sl=sl, g_off=g_off, GB_=GB_: e.tensor_tensor(
                    out=GB_[:, sl], in0=modB[:, g_off + hf * 512:g_off + (hf + 1) * 512], in1=pb[GBK[hf]][:], op=ALU.mult),
                    r=mtag(g_off) + [("pb", GBK[hf])], w=["GB"])
        P.barrier()
        build_rope()
        P.pool(lambda e: e.memset(KT[:], 0.0), w=["KT"])
        P.pool(lambda e: e.memset(VV[:], 0.0), w=["VVi"])
        P.pool(lambda e: e.memset(VV[:, :, 64:128], 1.0), r=["VVi"], w=["VVi"])

        P.phase = "P1"
        if "P1" in PH:
            norm_blocks(src, list(range(32)), s1T, sh1T, hT, lambda j, bb: bb * 128, lambda j, bb, e_: ("hT", bb, e_))
        P.barrier()

        if debug and "HT" in PH and l == 0:
            for k in range(8):
                P.dma("dbg", lambda e, k=k: e.dma_start(out=dbg[k * 128:(k + 1) * 128, :], in_=hT[:, k, :]), w=[("dbgd", k)])
        P.phase = "LAT"
        if "LAT" in PH:
            latent_unit(l)
        P.phase = "MLA"
        if "MLA" in PH:
            for h in range(4):
                mla_unit(l, h)
                if debug and "DMLA" in PH and l == 0 and h == 0:
                    P.barrier()
                    P.dma("dbg", lambda e: e.dma_start(out=dbg[0:128, :], in_=QT[:, :]), w=[("dbgd", 0)])
                    P.dma("dbg", lambda e: e.dma_start(out=dbg[128:256, :], in_=KT[:, PADK:PADK + S]), w=[("dbgd", 1)])
                    P.dma("dbg", lambda e: e.dma_start(out=dbg[256:384, :], in_=cqnT[:, 0, :]), w=[("dbgd", 2)])
                    P.dma("dbg", lambda e: e.dma_start(out=dbg[384:512, :], in_=cqnT[:, 1, :]), w=[("dbgd", 3)])
                    P.dma("dbg", lambda e: e.dma_start(out=dbg[512:640, :], in_=ckvnT[:, :]), w=[("dbgd", 4)])
                    P.dma("dbg", lambda e: e.dma_start(out=dbg[640:768, :], in_=VV[:, 0:32, :].rearrange("p a b -> p (a b)")), w=[("dbgd", 5)])
                    P.dma("dbg", lambda e: e.dma_start(out=dbg[768:896, 0:512], in_=rt1[:, :]), w=[("dbgd", 6)], eng="pool")
                    P.dma("dbg", lambda e: e.dma_start(out=dbg[768:896, 512:1024], in_=sqb[1][:, :]), w=[("dbgd", 8)])
                    P.dma("dbg", lambda e: e.dma_start(out=dbg[768:896, 1024:1536], in_=rt2[:, :]), w=[("dbgd", 9)], eng="pool")
                    P.dma("dbg", lambda e: e.dma_start(out=dbg[768:896, 1536:1538], in_=gq[:, :], allow_slow_non_contiguous=True), w=[("dbgd", 10)], eng="pool")
                    P.dma("dbg", lambda e: e.dma_start(out=dbg[768:896, 1538:1539], in_=gkv[:, :], allow_slow_non_contiguous=True), w=[("dbgd", 11)], eng="pool")
                    P.dma("dbg", lambda e: e.dma_start(out=dbg[768:896, 1540:1541], in_=epsb[:, :], allow_slow_non_contiguous=True), w=[("dbgd", 12)], eng="pool")
                    P.dma("dbg", lambda e: e.dma_start(out=dbg[896:1024, :], in_=TC[:, :]), w=[("dbgd", 7)])
                    P.barrier()
                    break
        P.barrier()
        P.phase = "A"
        if "A" in PH:
            for h in range(4):
                banded_unit(l, "A", h, 0, UA + h * 192, need_kv=(h % 2 == 0), first=True, last=True)
        P.phase = "D"
        if "D" in PH:
            for h in range(4):
                for g in range(3):
                    banded_unit(l, "D", h, g, UD + (h * 3 + g) * 192, need_kv=True, first=(g == 0), last=(g == 2))
        P.phase = "B"
        if "B" in PH:
            for ch in range(2):
                b_unit(l, ch)
        P.barrier()
        if "P3" not in PH:
            continue

        P.phase = "P3"
        for q4 in range(4):
            sv = acc[:, (q4 % 2) * 2048:(q4 % 2 + 1) * 2048].rearrange("p (a b) -> p a b", a=2)
            P.dma("ldwo%d" % (q4 % 2), lambda e, l=l, q4=q4, sv=sv: e.dma_start(
                out=sv, in_=w_out[l][q4 * 256:(q4 + 1) * 256, :].rearrange("(k p) n -> p k n", p=128)), w=[("accst", q4 % 2)])
            if q4 % 2 == 0:
                P.act(lambda e, q4=q4, sv=sv: e.copy(out=wo[:, q4 * 2:(q4 + 1) * 2, :], in_=sv), r=[("accst", q4 % 2)], w=[("wo", 0)])
            else:
                P.pool(lambda e, q4=q4, sv=sv: e.tensor_copy(out=wo[:, q4 * 2:(q4 + 1) * 2, :], in_=sv), r=[("accst", q4 % 2)], w=[("wo", 0)])
        if debug and l == 0 and "HT" not in PH:
            for k in range(8):
                P.dma("dbg", lambda e, k=k: e.dma_start(out=mixl[0][:, :, 0:512], in_=mixd[:, k * 512:(k + 1) * 512].rearrange("(c p) t -> p c t", p=128)),
                      r=[("mixl", 0)], w=[("mixl", 0)])
                P.dma("dbg", lambda e, k=k: e.dma_start(out=dbg[:, k * 512:(k + 1) * 512].rearrange("(c p) t -> p c t", p=128), in_=mixl[0][:, :, 0:512]),
                      r=[("mixl", 0)], w=[("mixl", 0)])
        def ldmix(b4):
            mi_ = b4 % 2
            P.dma("ldmix%d" % mi_, lambda e: e.dma_start(out=mixl[mi_][:], in_=mixd[:, b4 * 512:(b4 + 1) * 512].rearrange("(k p) t -> p k t", p=128)),
                  w=[("mixl", mi_)])
        ldmix(0)
        for b4 in range(8):
            mi = b4 % 2
            if b4 + 1 < 8:
                ldmix(b4 + 1)
            for bb in range(4):
                b = b4 * 4 + bb
                bk = (0, 1) if b % 2 == 0 else (3, 4)
                for hf in range(2):
                    ps = pb[bk[hf]][:]
                    for k in range(8):
                        P.pe(lambda e, k=k, ps=ps, hf=hf, mi=mi, bb=bb: e.matmul(ps, lhsT=mixl[mi][:, k, bb * 128:(bb + 1) * 128],
                                                                                  rhs=wo[:, k, hf * 512:(hf + 1) * 512],
                                                                                  start=(k == 0), stop=(k == 7)),
                             r=[("mixl", mi), ("wo", 0)], w=[("pb", bk[hf])])
                epilogue(pb[bk[0]][:], pb[bk[1]][:], (("pb", bk[0]), ("pb", bk[1])), src, xmid, b * 128, 128, G1B)
        P.barrier()

        if "P4" not in PH:
            continue
        P.phase = "P4"
        fcnt = [0]

        def fstage(src_ap, a_, b_, dst_ap, dtags, eng="pool"):
            i = fcnt[0] % 2
            fcnt[0] += 1
            sv = fst[i][:, 0:a_ * b_].rearrange("p (a b) -> p a b", a=a_)
            P.dma("ldf%d" % i, lambda e: e.dma_start(out=sv, in_=src_ap), w=[("fst", i)])
            if eng == "pool":
                P.pool(lambda e: e.tensor_copy(out=dst_ap, in_=sv), r=[("fst", i)], w=dtags)
            else:
                P.act(lambda e: e.copy(out=dst_ap, in_=sv), r=[("fst", i)], w=dtags)

        def fdma(src_ap, a_, b_):
            i = fcnt[0] % 2
            fcnt[0] += 1
            sv = fst[i][:, 0:a_ * b_].rearrange("p (a b) -> p a b", a=a_)
            P.dma("ldf%d" % i, lambda e: e.dma_start(out=sv, in_=src_ap), w=[("fst", i)])
            return i, sv

        def fcast(i, sv, dst_ap, dtags):
            P.act(lambda e: e.copy(out=dst_ap, in_=sv), r=[("fst", i)], w=dtags)

        items = [(gi_, cb_) for gi_ in range(4) for cb_ in range(6)]
        pend = {}

        def wstep(it, step):
            if it >= len(items):
                return
            gi_, cb_ = items[it]
            wi_ = it % 2
            ncols = min(512, DFF - cb_ * 512)

            def src(cbase, kh):
                return w_up[l][kh * 512:(kh + 1) * 512, cbase + cb_ * 512:cbase + cb_ * 512 + ncols].rearrange("(k p) n -> p k n", p=128)

            def dst(gv, kh):
                return wup[wi_][:, kh * 4:(kh + 1) * 4, gv * 512:gv * 512 + ncols]
            if step == 0:
                pend[(it, 0)] = [fdma(src(0, kh), 4, ncols) for kh in range(2)]
            elif step == 1:
                for kh, (i_, sv_) in enumerate(pend.pop((it, 0))):
                    fcast(i_, sv_, dst(0, kh), [("wup", wi_, 0)])
                pend[(it, 1)] = [fdma(src(DFF, kh), 4, ncols) for kh in range(2)]
            else:
                for kh, (i_, sv_) in enumerate(pend.pop((it, 1))):
                    fcast(i_, sv_, dst(1, kh), [("wup", wi_, 1)])

        for q11 in range(11):
            fstage(w_down[l][q11 * 256:(q11 + 1) * 256, :].rearrange("(c p) n -> p c n", p=128), 2, 1024,
                   wdn[:, q11 * 2:(q11 + 1) * 2, :], [("wdn", 0)], eng=("act" if q11 % 2 == 0 else "pool"))
        wdn_tags = [("wdn", 0)]
        P.pool(lambda e: e.memset(hT2[:, :, 0:1], 0.0), w=[("hT2", "z0")])
        P.pool(lambda e: e.memset(hT2[:, :, HT2C - 1:HT2C], 0.0), w=[("hT2", "z1")])
        groups = ((0, 1020), (1020, 2040), (2040, 3060), (3060, 4096))
        for st_ in range(3):
            wstep(0, st_)
        for gidx, (g0, g1) in enumerate(groups):
            blk0 = max(g0 - 1, 0) // 128
            blk1 = min((g1 + 1 + 127) // 128, 32)
            P.phase = "P4n"
            norm_blocks(xmid, list(range(blk0, blk1)), s2T, sh2T, hT2, lambda j, bb: 1 + j * 128, lambda j, bb, e_: ("hT2", j, e_))
            htags = [("hT2", "z0"), ("hT2", "z1")] + [("hT2", s_, e_) for s_ in range(blk1 - blk0) for e_ in (0, 1)]
            tiles = []
            o = g0
            while o < g1:
                n = min(510, g1 - o)
                tiles.append((o, n))
                o += n
            P.phase = "P4c"
            for c_ in range(22):
                cb4, c4 = divmod(c_, 4)
                it_ = gidx * 6 + cb4
                wi = it_ % 2
                nc4 = 4 if cb4 < 5 else 2
                if c4 == 0:
                    wstep(it_ + 1, 0)
                if c4 == 1:
                    wstep(it_ + 1, 1)
                if c4 == min(2, nc4 - 1):
                    wstep(it_ + 1, 2)
                for ti, (o0, n) in enumerate(tiles):
                    N = n + 2
                    cs = 1 + (o0 - 1) - blk0 * 128
                    par = ti % 2
                    bg, bv = (0, 1) if par == 0 else (3, 4)
                    psg, psv = pb[bg][:], pb[bv][:]
                    for (ps, wc, bank) in ((psg, c4 * 128, bg), (psv, 512 + c4 * 128, bv)):
                        for k in range(8):
                            P.pe(lambda e, k=k, ps=ps, wc=wc, cs=cs, N=N, wi=wi: e.matmul(ps[:, 0:N], lhsT=wup[wi][:, k, wc:wc + 128], rhs=hT2[:, k, cs:cs + N],
                                                                                        start=(k == 0), stop=(k == 7)),
                                 r=[("wup", wi, 0), ("wup", wi, 1)] + htags, w=[("pb", bank)])
                    for (ps, tt, fc, bank, nm) in ((psg, tg[par], c_, bg, "tg"), (psv, tv[par], 22 + c_, bv, "tv")):
                        P.act(lambda e, ps=ps, tt=tt, fc=fc, n=n: e.activation(out=tt[:, 0:n], in_=ps[:, 1:n + 1], func=AF.Identity, scale=fconv[:, fc, 1:2]),
                              r=[("pb", bank), "fconv"], w=[(nm, par)])
                        P.dve(lambda e, ps=ps, tt=tt, fc=fc, n=n: e.scalar_tensor_tensor(out=tt[:, 0:n], in0=ps[:, 0:n], scalar=fconv[:, fc, 0:1], in1=tt[:, 0:n],
                                                                                     op0=ALU.mult, op1=ALU.add), r=[("pb", bank), "fconv", (nm, par)], w=[(nm, par)])
                        P.dve(lambda e, ps=ps, tt=tt, fc=fc, n=n: e.scalar_tensor_tensor(out=tt[:, 0:n], in0=ps[:, 2:n + 2], scalar=fconv[:, fc, 2:3], in1=tt[:, 0:n],
                                                                                     op0=ALU.mult, op1=ALU.add), r=[("pb", bank), "fconv", (nm, par)], w=[(nm, par)])
                    P.act(lambda e, par=par, n=n: e.activation(out=tgg[par][:, 0:n], in_=tg[par][:, 0:n], func=AF.Gelu_apprx_tanh),
                          r=[("tg", par)], w=[("tgg", par)])
                    P.pool(lambda e, par=par, n=n, c_=c_, o0=o0, g0=g0: e.tensor_tensor(out=GT[:, c_, o0 - g0:o0 - g0 + n], in0=tgg[par][:, 0:n], in1=tv[par][:, 0:n], op=ALU.mult),
                           r=[("tgg", par), ("tv", par)], w=[("GT", c_)])
            gtags = [("GT", c_) for c_ in range(22)]
            P.phase = "P4d"
            t = g0
            wdi = 0
            while t < g1:
                m = min(128, g1 - t)
                bk = (5, 6) if wdi % 2 == 0 else (2, 3)
                wdi += 1
                for hf in range(2):
                    ps = pb[bk[hf]][:]
                    for c_ in range(22):
                        P.pe(lambda e, c_=c_, ps=ps, hf=hf, t=t, m=m, g0=g0: e.matmul(ps[0:m, :], lhsT=GT[:, c_, t - g0:t - g0 + m], rhs=wdn[:, c_, hf * 512:(hf + 1) * 512],
                                                                             start=(c_ == 0), stop=(c_ == 21)), r=gtags + wdn_tags, w=[("pb", bk[hf])])
                epilogue(pb[bk[0]][:], pb[bk[1]][:], (("pb", bk[0]), ("pb", bk[1])), xmid, out, t, m, G2B)
                t += m
        P.barrier()

    P.emit(nc, final_chans=("stx0", "stx1", "dbg"))
    return nc, P


_CACHE = {}


def _prep_shared(inputs):
    f = lambda a: np.ascontiguousarray(np.asarray(a, dtype=np.float32))
    w_in = f(inputs["w_in"])
    cols = _wu_cols()
    wu = np.ascontiguousarray(w_in[:, :, cols])
    sink = np.ascontiguousarray(np.broadcast_to(f(inputs["a_sink"])[:, None, :], (NL, 128, 4)))
    bconv = np.ascontiguousarray(f(inputs["b_conv"]).reshape(NL, 3, 2, 128).transpose(0, 3, 2, 1))
    gq = np.ascontiguousarray(f(inputs["c_norm_q"]).reshape(NL, 2, 128).transpose(0, 2, 1))
    gkv = np.ascontiguousarray(f(inputs["c_norm_kv"]).reshape(NL, 128, 1))
    uq = f(inputs["c_w_uq"])
    wq = np.empty((NL, 256, 4, 192), np.float32)
    for h in range(4):
        b0 = h * 96
        wq[:, :, h, 0:96] = uq[:, :, b0:b0 + 96]
        wq[:, :, h, 96:160] = uq[:, :, b0:b0 + 64]
        wq[:, :, h, 160:176] = uq[:, :, b0 + 80:b0 + 96]
        wq[:, :, h, 176:192] = uq[:, :, b0 + 64:b0 + 80]
    wq = np.ascontiguousarray(wq.reshape(NL, 256, 768))
    fconv = np.ascontiguousarray(f(inputs["ffn_conv"]).reshape(NL, 3, 44, 128).transpose(0, 3, 2, 1))
    bA, bD = _bias_tables(inputs["rel_bias"])
    half = 16
    inv_freq = (10000.0 ** (-np.arange(half, dtype=np.float32) / half)).astype(np.float32)
    cst = np.zeros((128, 4), np.float32)
    for p in list(range(0, 32)) + list(range(64, 96)):
        cst[p, 0] = inv_freq[(p % 32) % 16]
        cst[p, 1] = -1.0 if (p % 32) < 16 else 1.0
    return dict(
        cst=cst, idn=np.eye(128, dtype=np.float32),
        w_mod=f(inputs["w_mod"]), b_mod=f(inputs["b_mod"]).reshape(NL, 1, 6 * DM),
        norm_g=f(inputs["norm_g"]).reshape(NL, 1, 4 * DM), wu=wu, sink=sink, bconv=bconv, gq=gq, gkv=gkv, wq=wq,
        wkv=f(inputs["c_w_ukv"]), w_out=f(inputs["w_out"]), w_up=f(inputs["w_up"]), fconv=fconv,
        w_down=f(inputs["w_down"]), biasA=bA, biasD=bD)


def _core_inputs(inputs, shared, b):
    m = dict(shared)
    m["x"] = np.ascontiguousarray(np.asarray(inputs["x"][b], dtype=np.float32))
    m["cT"] = np.ascontiguousarray(np.asarray(inputs["c"][b], dtype=np.float32).reshape(8, 128).T)
    m["pos"] = np.ascontiguousarray(np.asarray(inputs["positions"][b], dtype=np.int32).reshape(1, S))
    return m


def kernel(**inputs):
    if "nc" not in _CACHE:
        _CACHE["nc"] = build(NL, False)[0]
    nc = _CACHE["nc"]
    shared = _prep_shared(inputs)
    in_maps = [_core_inputs(inputs, shared, b) for b in range(8)]
    res = run_bass_kernel_spmd(nc, in_maps, core_ids=list(range(8)))
    return np.stack([np.asarray(r["out"], dtype=np.float32) for r in res.results], axis=0)
```
